# Optimizing a Trainium2 kernel written in Bass

```python
import jax, jax.numpy as jnp
from jax import lax
import numpy as np

D_MODEL = 1024
BATCH = 8
SEQ = 8192
DEPTH = 1

HG_HEADS = 4
HG_DK = 128
HG_DV = 128
HG_WIDTH = HG_HEADS * HG_DK
HG_VWIDTH = HG_HEADS * HG_DV
HG_CHUNK = 32
ATT_HEADS = 8
ATT_KV_HEADS = 2
ATT_GROUP = ATT_HEADS // ATT_KV_HEADS
ATT_HD = 64
ATT_WIDTH = ATT_HEADS * ATT_HD
ATT_KV_WIDTH = ATT_KV_HEADS * ATT_HD
WINDOW = 128
D_FF = 2816
CONV_W = 3
EPS = 1e-6

HG_Q0 = 0
HG_F0 = HG_Q0 + HG_WIDTH
HG_I0 = HG_F0 + HG_WIDTH
HG_G0 = HG_I0 + HG_VWIDTH
AT_Q0 = HG_G0 + HG_VWIDTH
AT_K0 = AT_Q0 + ATT_WIDTH
AT_V0 = AT_K0 + ATT_KV_WIDTH
GATE_A0 = AT_V0 + ATT_KV_WIDTH
GATE_B0 = GATE_A0 + D_MODEL
IN_COLS = GATE_B0 + D_MODEL

kernel_name = "hgrn2_swa_sink_gated_hybrid"


def rms_norm(x, g):
    xf = x.astype(jnp.float32)
    y = xf * lax.rsqrt(jnp.mean(xf * xf, axis=-1, keepdims=True) + EPS)
    return y.astype(x.dtype) * g


def alibi_slopes(n_heads):
    return 2.0 ** (-8.0 * jnp.arange(1, n_heads + 1, dtype=jnp.float32) / n_heads)


def hgrn2_chunked(q, k, v, log_f):
    b_, s_, h_, dk = q.shape
    dv = v.shape[-1]
    n = s_ // HG_CHUNK
    q = q.reshape(b_, n, HG_CHUNK, h_, dk)
    k = k.reshape(b_, n, HG_CHUNK, h_, dk)
    v = v.reshape(b_, n, HG_CHUNK, h_, dv)
    cum = jnp.cumsum(log_f.reshape(b_, n, HG_CHUNK, h_, dk), axis=2)
    ref = cum[:, :, HG_CHUNK // 2 - 1:HG_CHUNK // 2]
    scores = jnp.einsum('bnthk,bnshk->bnhts', q * jnp.exp(cum - ref), k * jnp.exp(ref - cum))
    causal = jnp.tril(jnp.ones((HG_CHUNK, HG_CHUNK), dtype=bool))
    scores = jnp.where(causal, scores, 0.0)
    o_intra = jnp.einsum('bnhts,bnshv->bnthv', scores, v)
    cum_end = cum[:, :, -1]
    q_inter = q * jnp.exp(cum)
    k_inter = k * jnp.exp(cum_end[:, :, None] - cum)
    decay = jnp.exp(cum_end)

    def step(state, xs):
        qn, kn, vn, dn = xs
        o = jnp.einsum('bthk,bhkv->bthv', qn, state)
        state = dn[..., None] * state + jnp.einsum('bshk,bshv->bhkv', kn, vn)
        return state, o

    state0 = jnp.zeros((b_, h_, dk, dv), dtype=jnp.float32)
    xs = (jnp.moveaxis(q_inter, 1, 0), jnp.moveaxis(k_inter, 1, 0),
          jnp.moveaxis(v, 1, 0).astype(jnp.float32), jnp.moveaxis(decay, 1, 0))
    _, o_inter = lax.scan(step, state0, xs)
    o = o_intra + jnp.moveaxis(o_inter, 0, 1)
    return o.reshape(b_, s_, h_, dv)


def hgrn2_branch(q_pre, f_pre, i_pre, g_pre, lb, out_g):
    b_, s_, _ = q_pre.shape
    q = jax.nn.silu(q_pre.astype(jnp.float32)) * (HG_DK ** -0.5)
    f = lb + (1.0 - lb) * jax.nn.sigmoid(f_pre.astype(jnp.float32))
    k = 1.0 - f
    log_f = jnp.log(f)
    rs = lambda t, d: t.reshape(b_, s_, HG_HEADS, d)
    o = hgrn2_chunked(rs(q, HG_DK), rs(k, HG_DK), rs(i_pre, HG_DV), rs(log_f, HG_DK))
    o = rms_norm(o, out_g.astype(jnp.float32))
    g = rs(g_pre, HG_DV).astype(jnp.float32)
    o = o * jax.nn.silu(g)
    return o.reshape(b_, s_, HG_VWIDTH).astype(q_pre.dtype)


def swa_sink_attention(q, k, v, q_g, k_g, sinks):
    b_, s_, _ = q.shape
    n = s_ // WINDOW
    q = rms_norm(q.reshape(b_, n, WINDOW, ATT_KV_HEADS, ATT_GROUP, ATT_HD), q_g)
    k = rms_norm(k.reshape(b_, n, WINDOW, ATT_KV_HEADS, ATT_HD), k_g)
    v = v.reshape(b_, n, WINDOW, ATT_KV_HEADS, ATT_HD)
    prev = lambda t: jnp.pad(t, ((0, 0), (1, 0), (0, 0), (0, 0), (0, 0)))[:, :-1]
    k2 = jnp.concatenate([prev(k), k], axis=2)
    v2 = jnp.concatenate([prev(v), v], axis=2)
    scores = jnp.einsum('bnikgd,bnjkd->bkgnij', q, k2).astype(jnp.float32) * (ATT_HD ** -0.5)
    i_idx = jnp.arange(WINDOW)[:, None]
    j_idx = jnp.arange(2 * WINDOW)[None, :]
    delta = i_idx + WINDOW - j_idx
    band = (delta >= 0) & (delta < WINDOW)
    valid = band[None] & ((jnp.arange(n)[:, None, None] > 0) | (j_idx >= WINDOW)[None])
    slopes = alibi_slopes(ATT_HEADS).reshape(ATT_KV_HEADS, ATT_GROUP, 1, 1, 1)
    scores = jnp.where(valid, scores - slopes * delta.astype(jnp.float32), -jnp.inf)
    sink = sinks.astype(jnp.float32).reshape(ATT_KV_HEADS, ATT_GROUP, 1, 1, 1)
    m = jnp.maximum(jnp.max(scores, axis=-1, keepdims=True), sink)
    e = jnp.exp(scores - m)
    probs = e / (jnp.sum(e, axis=-1, keepdims=True) + jnp.exp(sink - m))
    out = jnp.einsum('bkgnij,bnjkd->bnikgd', probs.astype(v.dtype), v2)
    return out.reshape(b_, s_, ATT_WIDTH)


def conv_glu(h, w_up, conv_w, conv_b, w_down):
    u = h @ w_up
    gate, val = u[..., :D_FF], u[..., D_FF:]
    s_ = h.shape[1]
    gp = jnp.pad(gate, ((0, 0), (CONV_W - 1, 0), (0, 0)))
    conv = conv_b + sum(conv_w[tap] * gp[:, tap:tap + s_] for tap in range(CONV_W))
    return (jax.nn.gelu(conv, approximate=False) * val) @ w_down


def setup_inputs(seed: int = 0) -> dict:
    key = jax.random.key(seed)
    ks = jax.random.split(key, 16)
    nrm = lambda k, shape, fan_in: jax.random.normal(k, shape, jnp.float32) * fan_in ** -0.5
    return {
        "x": jax.random.normal(ks[0], (BATCH, SEQ, D_MODEL), jnp.float32),
        "norm1_g": 1.0 + 0.02 * jax.random.normal(ks[1], (DEPTH, D_MODEL), jnp.float32),
        "w_in": nrm(ks[2], (DEPTH, D_MODEL, IN_COLS), D_MODEL),
        "hgrn_lb_logits": 0.1 * jax.random.normal(ks[3], (DEPTH + 1, HG_WIDTH), jnp.float32),
        "hgrn_out_g": 1.0 + 0.02 * jax.random.normal(ks[4], (DEPTH, HG_DV), jnp.float32),
        "q_norm_g": 1.0 + 0.02 * jax.random.normal(ks[5], (DEPTH, ATT_HD), jnp.float32),
        "k_norm_g": 1.0 + 0.02 * jax.random.normal(ks[6], (DEPTH, ATT_HD), jnp.float32),
        "attn_sinks": 0.5 * jax.random.normal(ks[7], (DEPTH, ATT_HEADS), jnp.float32),
        "w_branch_a": nrm(ks[8], (DEPTH, HG_VWIDTH, D_MODEL), HG_VWIDTH),
        "w_branch_b": nrm(ks[9], (DEPTH, ATT_WIDTH, D_MODEL), ATT_WIDTH),
        "w_out": nrm(ks[10], (DEPTH, D_MODEL, D_MODEL), D_MODEL),
        "norm2_g": 1.0 + 0.02 * jax.random.normal(ks[11], (DEPTH, D_MODEL), jnp.float32),
        "w_up": nrm(ks[12], (DEPTH, D_MODEL, 2 * D_FF), D_MODEL),
        "conv_w": nrm(ks[13], (DEPTH, CONV_W, D_FF), CONV_W),
        "conv_b": 0.02 * jax.random.normal(ks[14], (DEPTH, D_FF), jnp.float32),
        "w_down": nrm(ks[15], (DEPTH, D_FF, D_MODEL), D_FF),
    }


def reference(x, norm1_g, w_in, hgrn_lb_logits, hgrn_out_g, q_norm_g, k_norm_g, attn_sinks,
              w_branch_a, w_branch_b, w_out, norm2_g, w_up, conv_w, conv_b, w_down):
    lb_table = jnp.cumsum(jax.nn.softmax(hgrn_lb_logits.astype(jnp.float32), axis=0), axis=0)
    for l in range(DEPTH):
        h = rms_norm(x, norm1_g[l])
        cols = h @ w_in[l]
        o_a = hgrn2_branch(cols[..., HG_Q0:HG_F0], cols[..., HG_F0:HG_I0],
                           cols[..., HG_I0:HG_G0], cols[..., HG_G0:AT_Q0],
                           lb_table[l], hgrn_out_g[l])
        o_b = swa_sink_attention(cols[..., AT_Q0:AT_K0], cols[..., AT_K0:AT_V0],
                                 cols[..., AT_V0:GATE_A0], q_norm_g[l], k_norm_g[l], attn_sinks[l])
        gate_a = jax.nn.sigmoid(cols[..., GATE_A0:GATE_B0])
        gate_b = jax.nn.sigmoid(cols[..., GATE_B0:IN_COLS])
        mixed = gate_a * (o_a @ w_branch_a[l]) + gate_b * (o_b @ w_branch_b[l])
        x = x + mixed @ w_out[l]
        h2 = rms_norm(x, norm2_g[l])
        x = x + conv_glu(h2, w_up[l], conv_w[l], conv_b[l], w_down[l])
    return x
```

```python
from contextlib import ExitStack

import numpy as np
import ml_dtypes

import concourse.bass as bass
import concourse.mybir as mybir
from concourse.bass_utils import run_bass_kernel_spmd

F32 = mybir.dt.float32
BF16 = mybir.dt.bfloat16
AF = mybir.ActivationFunctionType
ALU = mybir.AluOpType

N_CORES = 8
D = 1024
SEQ = 8192
T = 512
NB = T // 128
DFF = 2816
NFC = DFF // 128
EPS = 1e-6
NSLOT = 5
SAME_ENGINE_DIST = 4


def _needs_sync(p, o):
    if p.eng != o.eng:
        return True
    if o.eng in ("pe", "sp"):
        return False
    return (o.idx - p.idx) <= SAME_ENGINE_DIST

ENGS = ("pe", "act", "dve", "pool", "sp")


class Buf:
    __slots__ = ("name", "w", "r", "dsem", "dcount", "alias")

    def __init__(self, name):
        self.name = name
        self.w = None
        self.r = []
        self.dsem = None
        self.dcount = 0
        self.alias = []


class Op:
    __slots__ = ("eng", "fn", "deps", "signals", "sigval", "idx", "dma_ev")

    def __init__(self, eng, fn):
        self.eng = eng
        self.fn = fn
        self.deps = []
        self.signals = False
        self.sigval = 0
        self.idx = 0
        self.dma_ev = None


class Prog:
    def __init__(self):
        self.ops = {e: [] for e in ENGS}
        self.dma_sem_counts = []

    def _deps_for(self, reads, writes):
        deps = []
        for b in reads:
            if b.w is not None:
                deps.append(b.w)
        for b in writes:
            if b.w is not None:
                deps.append(b.w)
            deps.extend(b.r)
            for a in b.alias:
                if a.w is not None:
                    deps.append(a.w)
                deps.extend(a.r)
        return deps

    def _commit(self, ev, reads, writes):
        for b in reads:
            b.r.append(ev)
        for b in writes:
            b.w = ev
            b.r = []

    def op(self, eng, fn, reads=(), writes=()):
        o = Op(eng, fn)
        o.deps = self._deps_for(reads, writes)
        o.idx = len(self.ops[eng])
        self.ops[eng].append(o)
        self._commit(("op", o), reads, writes)
        return o

    def dma(self, fn, reads=(), writes=(), eng="sp", sem_buf=None):
        o = Op(eng, fn)
        b = sem_buf or (writes[0] if writes else reads[0])
        if b.dsem is None:
            b.dsem = len(self.dma_sem_counts)
            self.dma_sem_counts.append(0)
        deps = []
        for rb in reads:
            if rb.w is not None:
                deps.append(rb.w)
        for wb in writes:
            if wb.w is not None and not (wb.w[0] == "dma" and wb.w[1] == b.dsem):
                deps.append(wb.w)
            deps.extend(wb.r)
            for a in wb.alias:
                if a.w is not None:
                    deps.append(a.w)
                deps.extend(a.r)
        o.deps = deps
        o.idx = len(self.ops[eng])
        self.ops[eng].append(o)
        self.dma_sem_counts[b.dsem] += 16
        ev = ("dma", b.dsem, self.dma_sem_counts[b.dsem])
        o.dma_ev = ev
        self._commit(ev, reads, writes)
        return o

    def fence(self, engs=("pe", "act", "dve", "pool")):
        fb = [Buf("fence_" + e) for e in engs]
        for e, b in zip(engs, fb):
            self.op(e, lambda h: h.drain(), [], [b])
        for e in engs:
            self.op(e, lambda h: h.nop(), fb, [])

    def finalize(self):
        for e in ENGS:
            for o in self.ops[e]:
                for d in o.deps:
                    if d[0] == "op":
                        p = d[1]
                        if _needs_sync(p, o):
                            p.signals = True
        for e in ENGS:
            c = 0
            for o in self.ops[e]:
                if o.signals:
                    c += 1
                    o.sigval = c

    def emit(self, eng, h, esems, dsems):
        seen = {}
        for o in self.ops[eng]:
            need = {}
            for d in o.deps:
                if d[0] == "op":
                    p = d[1]
                    if not _needs_sync(p, o):
                        continue
                    key = ("e", p.eng)
                    val = p.sigval
                else:
                    key = ("d", d[1])
                    val = d[2]
                if val > need.get(key, 0):
                    need[key] = val
            for key, val in need.items():
                if seen.get(key, 0) >= val:
                    continue
                seen[key] = val
                sem = esems[key[1]] if key[0] == "e" else dsems[key[1]]
                h.wait_ge(sem, val)
            inst = o.fn(h)
            if o.dma_ev is not None:
                inst.then_inc(dsems[o.dma_ev[1]], 16)
            elif o.signals:
                inst.then_inc(esems[eng], 1)


def _host_consts():
    c = {}
    c["ident"] = np.eye(128, dtype=np.float32).astype(ml_dtypes.bfloat16)
    s = np.arange(128)[:, None]
    t = np.arange(128)[None, :]
    c["mask_st"] = (s <= t).astype(np.float32)
    c["ones_bf"] = np.ones((128, 128), dtype=np.float32).astype(ml_dtypes.bfloat16)
    blk = np.zeros((128, 128), dtype=np.float32)
    blk[:64, :64] = 1.0
    blk[64:, 64:] = 1.0
    c["blk64"] = blk.astype(ml_dtypes.bfloat16)
    oz = np.zeros((128, 2, 128), dtype=np.float32)
    oz[:, 0, :64] = 1.0
    oz[:, 1, 64:] = 1.0
    c["onesz"] = oz.astype(ml_dtypes.bfloat16)
    sm = np.ones((128, T), dtype=np.float32)
    sm[:, 0::128] = 0.0
    c["scanmask"] = sm
    dm = np.zeros((128, 2, 2, 4, 128), dtype=np.float64)
    for kv in range(2):
        for g in range(4):
            hh = kv * 4 + g
            slope = 2.0 ** (-8.0 * (hh + 1) / 8.0)
            d_cur = (t - s).astype(np.float64)
            d_prev = d_cur + 128.0
            dm[:, kv, 1, g, :] = np.where(d_cur >= 0, -slope * d_cur, -30000.0)
            dm[:, kv, 0, g, :] = np.where(d_prev < 128, -slope * d_prev, -30000.0)
    c["dm"] = dm.astype(np.float32)
    return c


(P_Q, P_G, P_F, P_I, P_AQ, P_AKV, P_GAB0, P_GAB1, P_GAB2, P_GAB3, P_WAB0, P_WAB1, P_WO0) = range(13)
P_GAB = (P_GAB0, P_GAB1, P_GAB2, P_GAB3)
P_WAB = (P_WAB0, P_WAB1)
P_UP0 = P_WO0 + 2
P_DN0 = P_UP0 + 11
NPIECE = P_DN0 + 6


def build_program(seq=SEQ, debug=False):
    ntile = seq // T
    nc = bass.Bass("TRN2", target_bir_lowering=False)
    dt_in = lambda name, shape, dt=F32: nc.dram_tensor(name, list(shape), dt, kind="ExternalInput").ap()
    x_d = dt_in("x", [seq, D])
    norm1_g = dt_in("norm1_g", [D])
    w_in = dt_in("w_in", [D, 4864])
    lb_logits = dt_in("hgrn_lb_logits", [2, 512])
    out_g = dt_in("hgrn_out_g", [128])
    qn_g = dt_in("q_norm_g", [64])
    kn_g = dt_in("k_norm_g", [64])
    sinks = dt_in("attn_sinks", [8])
    w_a = dt_in("w_branch_a", [512, D])
    w_b = dt_in("w_branch_b", [512, D])
    w_out = dt_in("w_out", [D, D])
    norm2_g = dt_in("norm2_g", [D])
    w_up = dt_in("w_up", [D, 2 * DFF])
    conv_w = dt_in("conv_w", [3, DFF])
    conv_b = dt_in("conv_b", [DFF])
    w_down = dt_in("w_down", [DFF, D])
    c_ident = dt_in("c_ident", [128, 128], BF16)
    c_mask = dt_in("c_mask_st", [128, 128])
    c_ones = dt_in("c_ones_bf", [128, 128], BF16)
    c_blk = dt_in("c_blk64", [128, 128], BF16)
    c_onesz = dt_in("c_onesz", [128, 2, 128], BF16)
    c_scan = dt_in("c_scanmask", [128, T])
    c_dm = dt_in("c_dm", [128, 2, 2, 4, 128])
    y_d = nc.dram_tensor("y", [seq, D], F32, kind="ExternalOutput").ap()
    wscr = nc.dram_tensor("wscr", [NPIECE, 128, 4096], BF16, kind="Internal").ap()
    dbg = {}
    if debug:
        dbg["x1"] = nc.dram_tensor("dbg_x1", [seq, D], F32, kind="ExternalOutput").ap()

    P = Prog()
    with ExitStack() as es:
        sb_total = [0]

        def sb(name, shape, dt):
            n = 1
            for s_ in shape[1:]:
                n *= s_
            sb_total[0] += n * (4 if dt == F32 else 2)
            return es.enter_context(nc.sbuf_tensor(name, list(shape), dt))

        X = [sb(f"X{i}", [128, NB, D], F32) for i in range(2)]
        hT = sb("hT", [128, 8, T], BF16)
        xs = [sb(f"xs{i}", [128, D], BF16) for i in range(2)]
        wslot = [sb(f"wslot{i}", [128, 4096], BF16) for i in range(NSLOT)]
        ident = sb("ident", [128, 128], BF16)
        mask_st = sb("mask_st", [128, 128], F32)
        ones_bf = sb("ones_bf", [128, 128], BF16)
        blk64 = sb("blk64", [128, 128], BF16)
        onesz = sb("onesz", [128, 2, 128], BF16)
        scanmask = sb("scanmask", [128, T], F32)
        dm = sb("dm", [128, 2, 2, 512], F32)
        g1 = sb("g1", [128, 8], F32)
        g2 = sb("g2", [128, 8], F32)
        lbl = sb("lbl", [128, 2, 4], F32)
        hco = sb("hco", [128, 4], F32)
        nhco = sb("nhco", [128, 4], F32)
        og = sb("og", [128, 1], F32)
        qg = sb("qg", [128, 1], F32)
        kg = sb("kg", [128, 1], F32)
        gqrow = sb("gqrow", [128, 64], F32)
        gkrow = sb("gkrow", [128, 64], F32)
        mq = sb("mq", [128, 4], F32)
        negshift = sb("negshift", [128, 1], F32)
        sinkraw = sb("sinkraw", [128, 4], F32)
        sinkexp = sb("sinkexp", [128, 4], F32)
        cw = sb("cw", [128, 3, NFC], F32)
        cb = sb("cb", [128, NFC], F32)
        epsc = sb("epsc", [128, 1], F32)
        mhalf = sb("mhalf", [128, 4], F32)
        ss = sb("ss", [128, NB], F32)
        ms = sb("ms", [128, NB], F32)
        rstd = sb("rstd", [128, NB], F32)
        nref = sb("nref", [128, 4, NB], F32)
        s1 = sb("s1", [128, 4, NB], F32)
        ddif = sb("ddif", [128, 4, NB], F32)
        s2 = sb("s2", [128, 4, NB], F32)
        dec = sb("dec", [128, 4, NB], F32)
        f32t = [sb(f"f32t{i}", [128, T], F32) for i in range(8)]
        tht = [f32t[0], f32t[1]]
        logf = [f32t[2], f32t[3]]
        cum = [f32t[4], f32t[5]]
        e1 = [f32t[6], f32t[7]]
        e3 = [f32t[2], f32t[3]]
        Ef = [f32t[2], f32t[3]]
        t1 = [f32t[4], f32t[5]]
        lnt = [f32t[0], f32t[1]]
        rden = f32t[7]
        bf32t = [Buf(f"f32t{i}") for i in range(8)]
        btht = [bf32t[0], bf32t[1]]
        blogf = [bf32t[2], bf32t[3]]
        bcum = [bf32t[4], bf32t[5]]
        be1 = [bf32t[6], bf32t[7]]
        be3 = [bf32t[2], bf32t[3]]
        bEf = [bf32t[2], bf32t[3]]
        bt1 = [bf32t[4], bf32t[5]]
        blnt = [bf32t[0], bf32t[1]]
        brden = bf32t[7]
        junk = f32t[1][:, :].bitcast(BF16)
        bjunk = bf32t[1]
        ARENA = 19472
        arena = sb("arena", [128, ARENA], BF16)

        def aview(off, n, dt, pat=None, **kw):
            v = arena[:, off:off + n]
            if dt == F32:
                v = v.bitcast(F32)
            if pat:
                v = v.rearrange(pat, **kw)
            return v

        O_VTM, O_QI, O_QD, O_KD, O_KI, O_ST, O_QS, O_GS, O_KF = 0, 2048, 4096, 6144, 8192, 10240, 11264, 13312, 15360
        qi = aview(O_QI, 2048, BF16, "p (h t) -> p h t", t=T)
        qd = aview(O_QD, 2048, BF16, "p (h t) -> p h t", t=T)
        kd = aview(O_KD, 2048, BF16, "p (h t) -> p h t", t=T)
        ki = aview(O_KI, 2048, BF16, "p (h t) -> p h t", t=T)
        qs = aview(O_QS, 2048, BF16, "p (h t) -> p h t", t=T)
        kf = aview(O_KF, 4096, F32, "p (h t) -> p h t", t=T)
        gs = aview(O_GS, 2048, BF16, "p (h t) -> p h t", t=T)
        vtm = aview(O_VTM, 2048, BF16, "p (b v) -> p b v", v=512)
        sT = [aview(O_ST + i * 512, 512, BF16, "p (h t) -> p h t", t=128) for i in range(2)]
        assert O_KF + 4096 <= ARENA
        act = aview(0, NFC * T, BF16, "p (c t) -> p c t", t=T)
        G = [aview(NFC * T + i * 2056, 2056, F32) for i in range(2)]
        c1 = [aview(NFC * T + 4112 + i * 1024, 1024, F32) for i in range(2)]
        c2 = [aview(NFC * T + 4112 + 2048 + i * 1024, 1024, F32) for i in range(2)]
        assert NFC * T + 4112 + 4096 <= ARENA

        kitm = [sb(f"kitm{i}", [128, 4, 128], BF16) for i in range(2)]
        S32 = sb("S32", [128, 4, 128], F32)
        Sb = sb("Sb", [128, 4, 128], BF16)
        sqo = [sb(f"sqo{i}", [128, 512], BF16) for i in range(2)]
        AT = sb("AT", [128, 4, T], BF16)
        qn = sb("qn", [128, 4, T], BF16)
        knz = [sb(f"knz{i}", [128, NB + 1, 128], BF16) for i in range(2)]
        vtz = [sb(f"vtz{i}", [128, NB + 1, 128], BF16) for i in range(2)]
        Ep = [sb(f"Ep{i}", [128, 512], BF16) for i in range(4)]
        BT = sb("BT", [128, 4, T], BF16)
        m1 = [sb(f"m1_{i}", [128, T], BF16) for i in range(2)]
        m2 = [sb(f"m2_{i}", [128, T], BF16) for i in range(2)]
        mixT = sb("mixT", [128, 8, T], BF16)
        gbuf = sb("gbuf", [128, 8, T], BF16)
        halo = sb("halo", [128, NFC, 2], F32)
        stage = [X[1][:, 0:2, :].rearrange("p b d -> p (b d)"), X[1][:, 2:4, :].rearrange("p b d -> p (b d)")]

        banks = [es.enter_context(nc.psum_tensor(f"bank{i}", [128, 512], F32)) for i in range(8)]

        def mk(name):
            return Buf(name)

        bX = [[mk(f"X{i}b{b}") for b in range(NB)] for i in range(2)]
        bhT = [mk(f"hT{b}") for b in range(NB)]
        bxs = [mk("xs0"), mk("xs1")]
        bslot = [mk(f"slot{i}") for i in range(NSLOT)]
        bconst = mk("const")
        bwscr = mk("wscr")
        bss, bms, brstd = mk("ss"), mk("ms"), mk("rstd")
        bqs = [mk(f"qs{h}") for h in range(4)]
        bkf = [mk(f"kf{h}") for h in range(4)]
        bgs = [mk(f"gs{h}") for h in range(4)]
        bvtm = [mk(f"vtm{b}") for b in range(NB)]
        bsm = [mk(f"small{h}") for h in range(4)]
        bqi = [mk(f"qi{h}") for h in range(4)]
        bqd = [mk(f"qd{h}") for h in range(4)]
        bkd = [mk(f"kd{h}") for h in range(4)]
        bki = [mk(f"ki{h}") for h in range(4)]
        bsT = [mk("sT0"), mk("sT1")]
        bkitm = [mk("kitm0"), mk("kitm1")]
        bS32 = [mk(f"S32_{h}") for h in range(4)]
        bSb = [mk(f"Sb{h}") for h in range(4)]
        bsqo = [mk("sqo0"), mk("sqo1")]
        bAT = [mk(f"AT{b}") for b in range(NB)]
        bqn = [mk(f"qn{j}") for j in range(4)]
        bkn = [mk(f"kn{b}") for b in range(NB + 1)]
        bvt = [mk(f"vt{b}") for b in range(NB + 1)]
        bEp = [mk(f"Ep{i}") for i in range(4)]
        bBT = [mk(f"BT{b}") for b in range(NB)]
        bm1 = [mk("m1_0"), mk("m1_1")]
        bm2 = [mk("m2_0"), mk("m2_1")]
        bmix = [mk(f"mix{c}") for c in range(8)]
        bgb = [mk(f"gb{c}") for c in range(8)]
        bG = [mk("G0"), mk("G1")]
        bc1 = [mk("c1_0"), mk("c1_1")]
        bc2 = [mk("c2_0"), mk("c2_1")]
        bhalo = [mk(f"halo{c}") for c in range(NFC)]
        bact = [mk(f"act{c}") for c in range(NFC)]
        bstage = [[bX[1][0], bX[1][1]], [bX[1][2], bX[1][3]]]
        bbank = [mk(f"bank{i}") for i in range(8)]
        by = mk("y")
        bdbg = mk("dbg")

        arena_map = []
        for h in range(4):
            arena_map += [(bqi[h], O_QI + h * 512, 512), (bqd[h], O_QD + h * 512, 512), (bkd[h], O_KD + h * 512, 512),
                          (bki[h], O_KI + h * 512, 512), (bqs[h], O_QS + h * 512, 512), (bkf[h], O_KF + h * 1024, 1024),
                          (bgs[h], O_GS + h * 512, 512)]
        for b in range(NB):
            arena_map.append((bvtm[b], O_VTM + b * 512, 512))
        for i in range(2):
            arena_map += [(bsT[i], O_ST + i * 512, 512), (bG[i], NFC * T + i * 2056, 2056),
                          (bc1[i], NFC * T + 4112 + i * 1024, 1024), (bc2[i], NFC * T + 4112 + 2048 + i * 1024, 1024)]
        for c in range(NFC):
            arena_map.append((bact[c], c * T, T))
        bstage23 = [mk("stage2"), mk("stage3")]
        arena_map += [(bstage23[0], 0, 4096), (bstage23[1], 4096, 4096)]
        for i_, (b1, o1, n1) in enumerate(arena_map):
            for (b2, o2, n2) in arena_map[i_ + 1:]:
                if o1 < o2 + n2 and o2 < o1 + n1:
                    b1.alias.append(b2)
                    b2.alias.append(b1)

        bank_ctr = [0]

        bank_reserved = set()

        def nextbank(reserve=False):
            while True:
                i = bank_ctr[0] % 8
                bank_ctr[0] += 1
                if i not in bank_reserved:
                    break
            if reserve:
                bank_reserved.add(i)
            return banks[i], bbank[i]

        def release_bank(bk):
            bank_reserved.discard(banks.index(bk))

        def A(out, in_, func, reads, writes, scale=1.0, bias=0.0, accum=None):
            if accum is None:
                P.op("act", lambda h: h.activation(out=out, in_=in_, func=func, scale=scale, bias=bias), reads, writes)
            else:
                P.op("act", lambda h: h.activation(out=out, in_=in_, func=func, scale=scale, bias=bias, accum_out=accum), reads, writes)

        def TS(eng, out, in0, s1_, s2_, op0, op1, reads, writes):
            if s2_ is None:
                P.op(eng, lambda h: h.tensor_scalar(out=out, in0=in0, scalar1=s1_, scalar2=None, op0=op0), reads, writes)
            else:
                P.op(eng, lambda h: h.tensor_scalar(out=out, in0=in0, scalar1=s1_, scalar2=s2_, op0=op0, op1=op1), reads, writes)

        def STT(out, in0, scalar, in1, op0, op1, reads, writes):
            P.op("dve", lambda h: h.scalar_tensor_tensor(out=out, in0=in0, scalar=scalar, in1=in1, op0=op0, op1=op1), reads, writes)

        def TT(eng, out, in0, in1, op, reads, writes):
            P.op(eng, lambda h: h.tensor_tensor(out=out, in0=in0, in1=in1, op=op), reads, writes)

        def CP(eng, out, in_, reads, writes):
            if eng == "act":
                P.op("act", lambda h: h.activation(out=out, in_=in_, func=AF.Copy), reads, writes)
            else:
                P.op(eng, lambda h: h.tensor_copy(out=out, in_=in_), reads, writes)

        def MM(out, lhsT, rhs, start, stop, reads, writes):
            P.op("pe", lambda h: h.matmul(out, lhsT=lhsT, rhs=rhs, start=start, stop=stop), reads, writes)

        def TR(out, in_, reads, writes):
            P.op("pe", lambda h: h.transpose(out=out, in_=in_, identity=ident[:]), list(reads) + [bconst], writes)

        def DMA(out, in_, reads, writes, sem_buf=None, slow=False):
            if slow:
                P.dma(lambda h: h.dma_start(out=out, in_=in_, allow_slow_non_contiguous=True), reads, writes, sem_buf=sem_buf)
            else:
                P.dma(lambda h: h.dma_start(out=out, in_=in_), reads, writes, sem_buf=sem_buf)

        dbg_list = []

        def DBG(name, ap, reads, dt=F32):
            if not debug:
                return
            shape = list(ap.shape)
            d_ = nc.dram_tensor("dbg_" + name, shape, dt, kind="ExternalOutput").ap()
            DMA(d_, ap, reads, [bdbg])
            dbg_list.append(name)

        DMA(ident[:], c_ident, [], [bconst])
        DMA(mask_st[:], c_mask, [], [bconst])
        DMA(ones_bf[:], c_ones, [], [bconst])
        DMA(blk64[:], c_blk, [], [bconst])
        DMA(onesz[:], c_onesz, [], [bconst])
        DMA(scanmask[:], c_scan, [], [bconst])
        DMA(dm[:], c_dm.rearrange("p a b g t -> p a b (g t)"), [], [bconst])
        DMA(g1[:], norm1_g.rearrange("(c p) -> p c", p=128), [], [bconst], slow=True)
        DMA(g2[:], norm2_g.rearrange("(c p) -> p c", p=128), [], [bconst], slow=True)
        DMA(lbl[:], lb_logits.rearrange("j (h p) -> p j h", p=128), [], [bconst], slow=True)
        DMA(og[:], out_g.rearrange("(p o) -> p o", o=1), [], [bconst], slow=True)
        for half in range(2):
            DMA(qg[half * 64:(half + 1) * 64, :], qn_g.rearrange("(p o) -> p o", o=1), [], [bconst], slow=True)
            DMA(kg[half * 64:(half + 1) * 64, :], kn_g.rearrange("(p o) -> p o", o=1), [], [bconst], slow=True)
            DMA(sinkraw[half * 64:(half + 1) * 64, :],
                sinks[half * 4:(half + 1) * 4].rearrange("(o g) -> o g", o=1).broadcast_to([64, 4]), [], [bconst], slow=True)
        DMA(gqrow[:], qn_g.rearrange("(o g) -> o g", o=1).broadcast_to([128, 64]), [], [bconst], slow=True)
        DMA(gkrow[:], kn_g.rearrange("(o g) -> o g", o=1).broadcast_to([128, 64]), [], [bconst], slow=True)
        DMA(cw[:], conv_w.rearrange("k (c p) -> p k c", p=128), [], [bconst], slow=True)
        DMA(cb[:], conv_b.rearrange("(c p) -> p c", p=128), [], [bconst], slow=True)

        bprm = mk("params")
        TT("dve", hco[:], lbl[:, 0, :], lbl[:, 1, :], ALU.subtract, [bconst], [bprm])
        A(hco[:], hco[:], AF.Tanh, [bprm], [bprm], scale=0.5)
        TS("dve", nhco[:], hco[:], 0.25, -0.25, ALU.mult, ALU.add, [bprm], [bprm])
        TS("dve", hco[:], nhco[:], -1.0, None, ALU.mult, None, [bprm], [bprm])
        TS("dve", og[:], og[:], 0.5, None, ALU.mult, None, [bconst], [bprm])
        TS("dve", qg[:], qg[:], 0.125, None, ALU.mult, None, [bconst], [bprm])
        for i_, grow in enumerate((gqrow, gkrow)):
            P.op("dve", lambda h, i_=i_, grow=grow: h.tensor_reduce(out=mq[:, i_:i_ + 1], in_=grow[:], op=ALU.max, axis=mybir.AxisListType.X), [bconst], [bprm])
            P.op("dve", lambda h, i_=i_, grow=grow: h.tensor_reduce(out=mq[:, 2 + i_:3 + i_], in_=grow[:], op=ALU.min, axis=mybir.AxisListType.X), [bconst], [bprm])
            TS("dve", mq[:, 2 + i_:3 + i_], mq[:, 2 + i_:3 + i_], -1.0, None, ALU.mult, None, [bprm], [bprm])
            TT("dve", mq[:, i_:i_ + 1], mq[:, i_:i_ + 1], mq[:, 2 + i_:3 + i_], ALU.max, [bprm], [bprm])
        STT(negshift[:], mq[:, 0:1], -8.0, mq[:, 1:2], ALU.mult, ALU.mult, [bprm], [bprm])
        A(sinkexp[:], sinkraw[:], AF.Exp, [bconst, bprm], [bprm], bias=negshift[:, 0:1])
        P.op("pool", lambda h: h.memset(epsc[:], EPS), [], [bprm])
        P.op("pool", lambda h: h.memset(mhalf[:], -0.5), [], [bprm])
        P.op("pool", lambda h: h.memset(S32[:], 0.0), [], bS32)
        P.op("pool", lambda h: h.memset(Sb[:], 0.0), [], bSb)
        P.op("pool", lambda h: h.memset(halo[:], 0.0), [], bhalo)
        for i_ in range(2):
            P.op("pool", lambda h, i_=i_: h.memset(knz[i_][:], 0.0), [], bkn)
            P.op("pool", lambda h, i_=i_: h.memset(vtz[i_][:], 0.0), [], bvt)

        cast_ctr = [0]
        bwp = [[mk(f"wscr{i}_{h}") for h in range(2)] for i in range(NPIECE)]
        stage4 = stage + [arena[:, 0:4096].bitcast(F32), arena[:, 4096:8192].bitcast(F32)]
        bstage4 = bstage + [[bstage23[0]], [bstage23[1]]]

        cast_jobs = []
        bslq = [[mk(f"slot{i}q{q}") for q in range(4)] for i in range(NSLOT)]

        def cast_half(pi, half, load_list, scale=None, perm=None):
            cast_jobs.append((pi, half, load_list, scale, perm))

        def cast_loads(k):
            pi, half, load_list, scale, perm = cast_jobs[k]
            st, bst = stage4[k % 4], bstage4[k % 4]
            st3 = st.rearrange("p (q n) -> p q n", q=4)
            for dstf, src in load_list:
                DMA(dstf(st3), src, [], bst, sem_buf=bst[0], slow=True)

        def cast_compute(k):
            pi, half, load_list, scale, perm = cast_jobs[k]
            st, bst = stage4[k % 4], bstage4[k % 4]
            sidx = k % NSLOT
            dst = wslot[sidx][:, 0:2048]
            for q in range(4):
                kc = half * 4 + q
                o_ap = dst[:, q * 512:(q + 1) * 512]
                i_ap = st[:, q * 512:(q + 1) * 512]
                if perm is not None:
                    o_ap, i_ap = perm(o_ap, i_ap)
                wr = [bslq[sidx][q]]
                if scale is not None and q % 2 == 0:
                    A(o_ap, i_ap, AF.Copy, bst + [bconst], wr, scale=scale[:, kc:kc + 1])
                elif scale is not None:
                    TS("dve", o_ap, i_ap, scale[:, kc:kc + 1], None, ALU.mult, None, bst + [bconst], wr)
                elif (k + q) % 2 == 0:
                    CP("dve", o_ap, i_ap, bst, wr)
                else:
                    CP("act", o_ap, i_ap, bst, wr)

        def cast_store(k):
            pi, half = cast_jobs[k][0], cast_jobs[k][1]
            sidx = k % NSLOT
            DMA(wscr[pi, :, half * 2048:(half + 1) * 2048], wslot[sidx][:, 0:2048], [bslot[sidx]] + bslq[sidx], [bwp[pi][half]], sem_buf=bslot[sidx])

        def run_cast_jobs(ahead=3):
            n = len(cast_jobs)
            for k in range(min(ahead, n)):
                cast_loads(k)
            for k in range(n):
                cast_compute(k)
                if k + ahead < n:
                    cast_loads(k + ahead)
                cast_store(k)

        def rows4(w, r0, c0, ncols, nq=4):
            return w[r0:r0 + nq * 128, c0:c0 + ncols].rearrange("(q p) n -> p q n", p=128)

        def std_piece(pi, w, c0, ncols, scale=None):
            for half in range(2):
                cast_half(pi, half, [(lambda st3: st3[:, :, 0:ncols], rows4(w, half * 512, c0, ncols))], scale)

        HG_Q0, HG_F0, HG_I0, HG_G0, AT_Q0, AT_K0, AT_V0, GA0, GB0 = 0, 512, 1024, 1536, 2048, 2560, 2688, 2816, 3840
        std_piece(P_Q, w_in, HG_Q0, 512, g1)
        std_piece(P_G, w_in, HG_G0, 512, g1)
        std_piece(P_F, w_in, HG_F0, 512, g1)
        std_piece(P_I, w_in, HG_I0, 512, g1)
        perm_q = lambda o_ap, i_ap: (o_ap.rearrange("p (j kv d) -> p j kv d", j=4, kv=2, d=64),
                                     i_ap.rearrange("p (kv j d) -> p j kv d", kv=2, j=4, d=64))
        for half in range(2):
            cast_half(P_AQ, half, [(lambda st3: st3[:, :, :], rows4(w_in, half * 512, AT_Q0, 512))], g1, perm=perm_q)
        std_piece(P_AKV, w_in, AT_K0, 256, g1)
        for hf in range(2):
            cast_half(P_WAB[hf], 0, [(lambda st3: st3[:, :, :], rows4(w_a, 0, hf * 512, 512))])
            ll = []
            for g in range(4):
                for kv in range(2):
                    r0 = (kv * 4 + g) * 64
                    ll.append((lambda st3, g=g, kv=kv: st3[kv * 64:(kv + 1) * 64, g, :], w_b[r0:r0 + 64, hf * 512:(hf + 1) * 512]))
            cast_half(P_WAB[hf], 1, ll)
        perm_pair = lambda o_ap, i_ap: (o_ap.rearrange("p (cc xy e) -> p cc xy e", cc=2, xy=2, e=128),
                                        i_ap.rearrange("p (xy cc e) -> p cc xy e", xy=2, cc=2, e=128))
        for k4 in range(4):
            for half in range(2):
                cast_half(P_GAB[k4], half,
                          [(lambda st3: st3[:, :, 0:256], rows4(w_in, half * 512, GA0 + k4 * 256, 256)),
                           (lambda st3: st3[:, :, 256:512], rows4(w_in, half * 512, GB0 + k4 * 256, 256))], g1, perm=perm_pair)
        std_piece(P_WO0, w_out, 0, 512)
        std_piece(P_WO0 + 1, w_out, 512, 512)
        for p in range(11):
            for half in range(2):
                cast_half(P_UP0 + p, half,
                          [(lambda st3: st3[:, :, 0:256], rows4(w_up, half * 512, p * 256, 256)),
                           (lambda st3: st3[:, :, 256:512], rows4(w_up, half * 512, DFF + p * 256, 256))], g2, perm=perm_pair)
        for hf in range(2):
            for kgp in range(3):
                pi = P_DN0 + hf * 3 + kgp
                nk = 8 if kgp < 2 else NFC - 16
                for half in range(2):
                    nq = min(4, nk - half * 4)
                    if nq > 0:
                        cast_half(pi, half, [(lambda st3, nq=nq: st3[:, 0:nq, :], rows4(w_down, (kgp * 8 + half * 4) * 128, hf * 512, 512, nq))])

        run_cast_jobs()

        stream_issued = [0]
        total_stream = ntile * NPIECE

        def issue_loads_upto(gidx):
            while stream_issued[0] <= min(gidx, total_stream - 1):
                k = stream_issued[0]
                DMA(wslot[k % NSLOT][:, :], wscr[k % NPIECE], bwp[k % NPIECE], [bslot[k % NSLOT]])
                stream_issued[0] += 1

        last_piece = [-1]

        def piece(ti, pi, hold=None):
            gidx = ti * NPIECE + pi
            assert gidx >= last_piece[0]
            last_piece[0] = gidx
            oldest = gidx if hold is None else ti * NPIECE + hold
            assert gidx <= oldest + NSLOT - 1
            issue_loads_upto(oldest + NSLOT - 1)
            return wslot[gidx % NSLOT], bslot[gidx % NSLOT]

        bss_b = [mk(f"ss{b}") for b in range(NB)]
        brstd_b = [mk(f"rstd{b}") for b in range(NB)]
        by_blk = [[mk(f"y{i}_{b}") for b in range(NB)] for i in range(2)]

        def load_x(ti):
            xb = ti % 2
            for b in range(NB):
                r0 = ti * T + b * 128
                DMA(X[xb][:, b, :], x_d[r0:r0 + 128, :], [], [bX[xb][b]])

        def norm_stats(xb, b):
            A(junk, X[xb][:, b, :], AF.Square, [bX[xb][b]], [bjunk, bss_b[b]], accum=ss[:, b:b + 1])
            TS("dve", ms[:, b:b + 1], ss[:, b:b + 1], 1.0 / D, EPS, ALU.mult, ALU.add, [bss_b[b]], [bss_b[b]])
            TT("pool", rstd[:, b:b + 1], ms[:, b:b + 1], mhalf[:, 0:1], ALU.pow, [bss_b[b], bprm], [brstd_b[b]])
            k = b % 2
            TS("dve", xs[k][:], X[xb][:, b, :], rstd[:, b:b + 1], None, ALU.mult, None, [bX[xb][b], brstd_b[b]], [bxs[k]])

        def norm_transpose(xb, b):
            k = b % 2
            for half in range(2):
                bk, bbk = nextbank()
                for c4 in range(4):
                    c = half * 4 + c4
                    MM(bk[:, c4 * 128:(c4 + 1) * 128], xs[k][:, c * 128:(c + 1) * 128], ident[:], True, True, [bxs[k], bconst], [bbk])
                CP("act", hT[:, half * 4:(half + 1) * 4, b * 128:(b + 1) * 128], bk[:].rearrange("p (c t) -> p c t", t=128), [bbk], [bhT[b]])

        def norm_block(xb, b):
            norm_stats(xb, b)
            norm_transpose(xb, b)

        def norm1_stream(ti):
            for b in range(NB):
                norm_block(ti % 2, b)
                yield

        def fm_group(W, bW, c0, ncol=128, kcs=8, wstride=512):
            bk, bbk = nextbank()
            for kc in range(kcs):
                MM(bk[0:ncol, :], W[:, kc * wstride + c0: kc * wstride + c0 + ncol], hT[:, kc, :], kc == 0, kc == kcs - 1,
                   [bW] + bhT, [bbk])
            return bk, bbk

        def merge(*gens, ratio=None, delay=None):
            gens = list(gens)
            ratio = ratio or [1] * len(gens)
            delay = list(delay or [0] * len(gens))
            alive = [True] * len(gens)
            while any(alive):
                for i, g in enumerate(gens):
                    if delay[i] > 0:
                        delay[i] -= 1
                        continue
                    for _ in range(ratio[i]):
                        if alive[i]:
                            try:
                                next(g)
                            except StopIteration:
                                alive[i] = False

        def run(g):
            for _ in g:
                pass

        def phase_tanh(ti):
            W, bW = piece(ti, P_Q)
            for h in range(4):
                bk, bbk = fm_group(W, bW, h * 128)
                k = h % 2
                A(tht[k][:], bk[:], AF.Tanh, [bbk], [btht[k]], scale=0.5)
                STT(qs[:, h, :], tht[k][:], 1.0, bk[:], ALU.add, ALU.mult, [btht[k], bbk], [bqs[h]])
            W, bW = piece(ti, P_G)
            for h in range(4):
                bk, bbk = fm_group(W, bW, h * 128)
                k = h % 2
                A(tht[k][:], bk[:], AF.Tanh, [bbk], [btht[k]], scale=0.5)
                STT(gs[:, h, :], tht[k][:], 1.0, bk[:], ALU.add, ALU.mult, [btht[k], bbk], [bgs[h]])
            W, bW = piece(ti, P_F)
            for h in range(4):
                bk, bbk = fm_group(W, bW, h * 128)
                k = h % 2
                A(tht[k][:], bk[:], AF.Tanh, [bbk], [btht[k]], scale=0.5)
                TS("dve", kf[:, h, :], tht[k][:], nhco[:, h:h + 1], hco[:, h:h + 1], ALU.mult, ALU.add, [btht[k], bprm], [bkf[h]])
            if ti == 0:
                DBG("qs", qs, bqs, BF16)
                DBG("kf", kf, bkf)
                DBG("gs", gs, bgs, BF16)

        def proj2_stream(ti):
            W, bW = piece(ti, P_I)
            for b in range(NB):
                bk, bbk = nextbank()
                for kc in range(8):
                    MM(bk[:], hT[:, kc, b * 128:(b + 1) * 128], W[:, kc * 512:(kc + 1) * 512], kc == 0, kc == 7, [bW, bhT[b]], [bbk])
                CP("act", vtm[:, b, :], bk[:], [bbk], [bvtm[b]])
                yield
            Wq, bWq = piece(ti, P_AQ)
            for j in range(5):
                k = j % 2
                if j < 4:
                    bk, bbk = fm_group(Wq, bWq, j * 128)
                else:
                    Wk, bWk = piece(ti, P_AKV)
                    bk, bbk = fm_group(Wk, bWk, 0)
                A(sqo[k][:], bk[:], AF.Square, [bbk], [bsqo[k]])
                bk2, bbk2 = nextbank()
                MM(bk2[:], blk64[:], sqo[k][:], True, True, [bconst, bsqo[k]], [bbk2])
                yield
                A(lnt[k][:], bk2[:], AF.Ln, [bbk2, bprm], [blnt[k]], scale=1.0 / 64, bias=epsc[:, 0:1])
                A(lnt[k][:], lnt[k][:], AF.Exp, [blnt[k]], [blnt[k]], scale=-0.5)
                if j < 4:
                    STT(qn[:, j, :], bk[:], qg[:, 0:1], lnt[k][:], ALU.mult, ALU.mult, [bbk, blnt[k], bprm], [bqn[j]])
                else:
                    for kv in range(2):
                        ps_ = slice(kv * 64, (kv + 1) * 64)
                        STT(knz[kv][ps_, 1:NB + 1, :], bk[ps_, :].rearrange("p (b t) -> p b t", t=128), kg[ps_, 0:1],
                            lnt[k][ps_, :].rearrange("p (b t) -> p b t", t=128), ALU.mult, ALU.mult, [bbk, blnt[k], bconst], bkn[1:])
                yield
            for b in range(NB):
                bk, bbk = nextbank()
                for kc in range(8):
                    MM(bk[:, 0:128], hT[:, kc, b * 128:(b + 1) * 128], Wk[:, kc * 512 + 128: kc * 512 + 256], kc == 0, kc == 7, [bWk, bhT[b]], [bbk])
                for kv in range(2):
                    CP("act", vtz[kv][:, b + 1, kv * 64:(kv + 1) * 64], bk[:, kv * 64:(kv + 1) * 64], [bbk], [bvt[b + 1]])
                if b % 2 == 1:
                    yield
            if ti == 0:
                DBG("vtm", vtm, bvtm, BF16)
                DBG("qn", qn[:], bqn, BF16)
                for kv in range(2):
                    DBG(f"knz{kv}", knz[kv][:], bkn, BF16)
                    DBG(f"vtz{kv}", vtz[kv][:], bvt, BF16)

        def algebra_stream(ti, heads=(0, 1, 2, 3)):
            for h in heads:
                k = h % 2
                A(logf[k][:], kf[:, h, :], AF.Ln, [bkf[h]], [blogf[k]], scale=-1.0, bias=1.0)
                P.op("dve", lambda hh, k=k: hh.tensor_tensor_scan(out=cum[k][:], data0=scanmask[:], data1=logf[k][:], initial=0.0,
                                                                   op0=ALU.mult, op1=ALU.add), [bconst, blogf[k]], [bcum[k]])
                yield
                cum3 = cum[k][:].rearrange("p (b t) -> p b t", t=128)
                TS("dve", nref[:, h, :], cum3[:, :, 63], -1.0, None, ALU.mult, None, [bcum[k]], [bsm[h]])
                TT("dve", ddif[:, h, :], cum3[:, :, 127], cum3[:, :, 63], ALU.subtract, [bcum[k]], [bsm[h]])
                A(e1[k][:], cum[k][:], AF.Exp, [bcum[k]], [be1[k]])
                A(s1[:, h, :], nref[:, h, :], AF.Exp, [bsm[h]], [bsm[h]])
                A(s2[:, h, :], ddif[:, h, :], AF.Exp, [bsm[h]], [bsm[h]])
                A(dec[:, h, :], cum3[:, :, 127], AF.Exp, [bcum[k]], [bsm[h]])
                STT(qi[:, h, :], qs[:, h, :], 0.5 * (128.0 ** -0.5), e1[k][:], ALU.mult, ALU.mult, [bqs[h], be1[k]], [bqi[h]])
                yield
                for b in range(NB):
                    sl = slice(b * 128, (b + 1) * 128)
                    A(e3[k][:, sl], cum[k][:, sl], AF.Exp, [bcum[k]], [be3[k]], scale=-1.0, bias=cum[k][:, b * 128 + 63: b * 128 + 64])
                for b in range(NB):
                    sl = slice(b * 128, (b + 1) * 128)
                    TS("dve", qd[:, h, sl], qi[:, h, sl], s1[:, h, b:b + 1], None, ALU.mult, None, [bqi[h], bsm[h]], [bqd[h]])
                yield
                TT("dve", kd[:, h, :], kf[:, h, :], e3[k][:], ALU.mult, [bkf[h], be3[k]], [bkd[h]])
                for b in range(NB):
                    sl = slice(b * 128, (b + 1) * 128)
                    TS("dve", ki[:, h, sl], kd[:, h, sl], s2[:, h, b:b + 1], None, ALU.mult, None, [bkd[h], bsm[h]], [bki[h]])
                yield
            if ti == 0 and heads[-1] == 3:
                DBG("qi", qi, bqi, BF16)
                DBG("qd", qd, bqd, BF16)
                DBG("kd", kd, bkd, BF16)
                DBG("ki", ki, bki, BF16)
                DBG("dec", dec[:], bsm)

        hg_bkU = [None] * NB

        def hgA_stream(ti, blocks=(0, 1, 2, 3)):
            for b in blocks:
                sl = slice(b * 128, (b + 1) * 128)
                k = b % 2
                bkS, bbkS = nextbank()
                for h in range(4):
                    MM(bkS[:, h * 128:(h + 1) * 128], kd[:, h, sl], qd[:, h, sl], True, True, [bkd[h], bqd[h]], [bbkS])
                bkT, bbkT = nextbank()
                pv = bkT[:].bitcast(BF16)
                for h in range(4):
                    TR(pv[:, h * 128:(h + 1) * 128], ki[:, h, sl], [bki[h]], [bbkT])
                yield
                for h in range(4):
                    TT("dve", sT[k][:, h, :], bkS[:, h * 128:(h + 1) * 128], mask_st[:], ALU.mult, [bbkS, bconst], [bsT[k]])
                CP("act", kitm[k][:], pv[:, 0:512].rearrange("p (h t) -> p h t", t=128), [bbkT], [bkitm[k]])
                bkU, bbkU = nextbank(reserve=True)
                for h in range(4):
                    MM(bkU[:, h * 128:(h + 1) * 128], kitm[k][:, h, :], vtm[:, b, h * 128:(h + 1) * 128], True, True, [bkitm[k], bvtm[b]], [bbkU])
                hg_bkU[b] = (bkU, bbkU)
                yield

        def hgB_stream(ti):
            for b in range(NB):
                sl = slice(b * 128, (b + 1) * 128)
                k = b % 2
                bkU, bbkU = hg_bkU[b]
                bkO, bbkO = nextbank()
                for h in range(4):
                    MM(bkO[:, h * 128:(h + 1) * 128], vtm[:, b, h * 128:(h + 1) * 128], sT[k][:, h, :], True, False, [bvtm[b], bsT[k]], [bbkO])
                    MM(bkO[:, h * 128:(h + 1) * 128], Sb[:, h, :], qi[:, h, sl], False, True, [bSb[h], bqi[h]], [bbkO])
                for h in range(4):
                    STT(S32[:, h, :], S32[:, h, :], dec[:, h, b:b + 1], bkU[:, h * 128:(h + 1) * 128], ALU.mult, ALU.add,
                        [bS32[h], bsm[h], bbkU], [bS32[h]])
                release_bank(bkU)
                CP("dve", Sb[:], S32[:], bS32, bSb)
                A(sqo[k][:], bkO[:], AF.Square, [bbkO], [bsqo[k]])
                bkN, bbkN = nextbank()
                MM(bkN[:], ones_bf[:], sqo[k][:], True, True, [bconst, bsqo[k]], [bbkN])
                yield
                A(lnt[k][:], bkN[:], AF.Ln, [bbkN, bprm], [blnt[k]], scale=1.0 / 128, bias=epsc[:, 0:1])
                A(lnt[k][:], lnt[k][:], AF.Exp, [blnt[k]], [blnt[k]], scale=-0.5)
                STT(t1[k][:], bkO[:], og[:, 0:1], lnt[k][:], ALU.mult, ALU.mult, [bbkO, blnt[k], bprm], [bt1[k]])
                TT("pool", AT[:, :, sl], t1[k][:].rearrange("p (h t) -> p h t", t=128), gs[:, :, sl], ALU.mult, [bt1[k]] + bgs, [bAT[b]])
                yield
            if ti == 0:
                DBG("AT", AT[:], bAT, BF16)

        def swa_block_stream(ti):
            for b in range(NB):
                sl = slice(b * 128, (b + 1) * 128)
                gb = ti * NB + b
                kbs = (1,) if gb == 0 else (0, 1)
                for kv in range(2):
                    for kb in kbs:
                        bkC, bbkC = nextbank()
                        MM(bkC[:].rearrange("p (g t) -> p g t", t=128), knz[kv][:, b + kb, :], qn[:, :, sl], True, True, [bkn[b + kb]] + bqn, [bbkC])
                        e = kb % 2
                        TT("dve", Ef[e][:], bkC[:], dm[:, kv, kb, :], ALU.add, [bbkC, bconst], [bEf[e]])
                        A(Ep[kv * 2 + kb][:], Ef[e][:], AF.Exp, [bEf[e], bprm], [bEp[kv * 2 + kb]], bias=negshift[:, 0:1])
                    yield
                bkP, bbkP = nextbank()
                bkD, bbkD = nextbank()
                pairs = [(kv, kb) for kv in range(2) for kb in kbs]
                for i_, (kv, kb) in enumerate(pairs):
                    MM(bkP[:], vtz[kv][:, b + kb, :], Ep[kv * 2 + kb][:], i_ == 0, i_ == len(pairs) - 1, [bvt[b + kb], bEp[kv * 2 + kb]], [bbkP])
                for i_, (kv, kb) in enumerate(pairs):
                    MM(bkD[:], onesz[:, kv, :], Ep[kv * 2 + kb][:], i_ == 0, i_ == len(pairs) - 1, [bconst, bEp[kv * 2 + kb]], [bbkD])
                yield
                for g in range(4):
                    gsl = slice(g * 128, (g + 1) * 128)
                    A(rden[:, gsl], bkD[:, gsl], AF.Ln, [bbkD, bprm], [brden], bias=sinkexp[:, g:g + 1])
                A(rden[:], rden[:], AF.Exp, [brden], [brden], scale=-1.0)
                TT("dve", BT[:, :, sl], bkP[:].rearrange("p (g t) -> p g t", t=128), rden[:].rearrange("p (g t) -> p g t", t=128),
                   ALU.mult, [bbkP, brden], [bBT[b]])
                yield
            if ti == 0:
                DBG("BT", BT[:], bBT, BF16)

        def gates_stream(ti):
            for k4 in range(4):
                Wg, bWg = piece(ti, P_GAB[k4])
                for cc in range(2):
                    dc = 2 * k4 + cc
                    bkGa, bbkGa = fm_group(Wg, bWg, (2 * cc) * 128)
                    CP("act", mixT[:, dc, :], bkGa[:], [bbkGa], [bmix[dc]])
                    yield
                    bkGb, bbkGb = fm_group(Wg, bWg, (2 * cc + 1) * 128)
                    CP("dve", gbuf[:, dc, :], bkGb[:], [bbkGb], [bgb[dc]])
                    yield

        def branches(ti):
            for hf in range(2):
                Wab, bWab = piece(ti, P_WAB[hf])
                for c4 in range(4):
                    dc = hf * 4 + c4
                    k = dc % 2
                    A(tht[0][:], mixT[:, dc, :], AF.Tanh, [bmix[dc]], [btht[0]], scale=0.5)
                    A(tht[1][:], gbuf[:, dc, :], AF.Tanh, [bgb[dc]], [btht[1]], scale=0.5)
                    bkA, bbkA = nextbank()
                    for h in range(4):
                        MM(bkA[:], Wab[:, h * 512 + c4 * 128: h * 512 + (c4 + 1) * 128], AT[:, h, :], h == 0, h == 3, [bWab] + bAT, [bbkA])
                    bkB, bbkB = nextbank()
                    for g in range(4):
                        MM(bkB[:], Wab[:, (4 + g) * 512 + c4 * 128: (4 + g) * 512 + (c4 + 1) * 128], BT[:, g, :], g == 0, g == 3, [bWab] + bBT, [bbkB])
                    STT(m1[k][:], tht[0][:], 1.0, bkA[:], ALU.add, ALU.mult, [btht[0], bbkA], [bm1[k]])
                    STT(m2[k][:], tht[1][:], 1.0, bkB[:], ALU.add, ALU.mult, [btht[1], bbkB], [bm2[k]])
                    TT("pool", mixT[:, dc, :], m1[k][:], m2[k][:], ALU.add, [bm1[k], bm2[k]], [bmix[dc]])
            if ti == 0:
                DBG("mixT", mixT[:], bmix, BF16)

        def wout_norm2(ti):
            xb = ti % 2
            W0, bW0 = piece(ti, P_WO0)
            W1, bW1 = piece(ti, P_WO0 + 1, hold=P_WO0)
            for b in range(NB):
                for hf, (W, bW) in enumerate(((W0, bW0), (W1, bW1))):
                    bk, bbk = nextbank()
                    for dc in range(8):
                        MM(bk[:], mixT[:, dc, b * 128:(b + 1) * 128], W[:, dc * 512:(dc + 1) * 512], dc == 0, dc == 7, [bW, bmix[dc]], [bbk])
                    xsl = X[xb][:, b, hf * 512:(hf + 1) * 512]
                    STT(xsl, bk[:], 0.5, xsl, ALU.mult, ALU.add, [bbk, bX[xb][b]], [bX[xb][b]])
                norm_stats(xb, b)
                if b >= 1:
                    norm_transpose(xb, b - 1)
            norm_transpose(xb, NB - 1)
            if debug:
                for b in range(NB):
                    r0 = ti * T + b * 128
                    DMA(dbg["x1"][r0:r0 + 128, :], X[xb][:, b, :], [bX[xb][b]], [bdbg])

        def ffn_up_chunk(ti, W, bW, c, cc):
            k = c % 2
            bkG, bbkG = fm_group(W, bW, (2 * cc) * 128)
            bkV, bbkV = fm_group(W, bW, (2 * cc + 1) * 128)
            CP("act", G[k][:, 2:T + 2], bkG[:], [bbkG], [bG[k]])
            CP("pool", G[k][:, 0:2], halo[:, c, :], [bhalo[c]], [bG[k]])
            TS("dve", c1[k], G[k][:, 0:T], cw[:, 0, c:c + 1], cb[:, c:c + 1], ALU.mult, ALU.add, [bG[k], bconst], [bc1[k]])
            STT(c2[k], G[k][:, 1:T + 1], cw[:, 1, c:c + 1], c1[k], ALU.mult, ALU.add, [bG[k], bc1[k], bconst], [bc2[k]])
            STT(c1[k], bkG[:], cw[:, 2, c:c + 1], c2[k], ALU.mult, ALU.add, [bbkG, bc2[k], bconst], [bc1[k]])
            CP("pool", halo[:, c, :], G[k][:, T:T + 2], [bG[k]], [bhalo[c]])
            A(c2[k], c1[k], AF.Gelu, [bc1[k]], [bc2[k]])
            TT("dve", act[:, c, :], c2[k], bkV[:], ALU.mult, [bc2[k], bbkV], [bact[c]])

        def ffn_up(ti, pieces):
            for p in pieces:
                W, bW = piece(ti, P_UP0 + p)
                for cc in range(2):
                    ffn_up_chunk(ti, W, bW, 2 * p + cc, cc)

        def ffn_down_stream(ti, hold_first=None):
            xb = ti % 2
            for hf in range(2):
                bks = [nextbank(reserve=True) for _ in range(NB)]
                for kgp in range(3):
                    W, bW = piece(ti, P_DN0 + hf * 3 + kgp, hold=hold_first if (hf == 0 and kgp == 0) else None)
                    nk = 8 if kgp < 2 else NFC - 16
                    for b in range(NB):
                        for kc in range(nk):
                            c = kgp * 8 + kc
                            MM(bks[b][0][:], act[:, c, b * 128:(b + 1) * 128], W[:, kc * 512:(kc + 1) * 512], c == 0, c == NFC - 1, [bW, bact[c]], [bks[b][1]])
                        if not (hf == 1 and kgp == 2):
                            yield
                for b in range(NB):
                    xsl = X[xb][:, b, hf * 512:(hf + 1) * 512]
                    TT("dve", xsl, bks[b][0][:], xsl, ALU.add, [bks[b][1], bX[xb][b]], [bX[xb][b]])
                    release_bank(bks[b][0])
            for b in range(NB):
                r0 = ti * T + b * 128
                DMA(y_d[r0:r0 + 128, :], X[xb][:, b, :], [bX[xb][b]], [by_blk[xb][b]], sem_buf=bX[xb][b])

        load_x(0)
        run(norm1_stream(0))
        for ti in range(ntile):
            if ti + 1 < ntile:
                load_x(ti + 1)
            if ti > 0:
                for kv in range(2):
                    CP("pool", knz[kv][:, 0, :], knz[kv][:, NB, :], [bkn[NB]], [bkn[0]])
                    CP("pool", vtz[kv][:, 0, :], vtz[kv][:, NB, :], [bvt[NB]], [bvt[0]])
            phase_tanh(ti)
            merge(proj2_stream(ti), algebra_stream(ti, (0, 2)), algebra_stream(ti, (1, 3)))
            merge(hgA_stream(ti), hgB_stream(ti), swa_block_stream(ti), gates_stream(ti), delay=[0, 2, 0, 0])
            branches(ti)
            wout_norm2(ti)
            ffn_up(ti, range(10))
            W10, bW10 = piece(ti, P_UP0 + 10)
            gd = ffn_down_stream(ti, hold_first=P_UP0 + 10)
            for _ in range(NB):
                next(gd)
            ffn_up_chunk(ti, W10, bW10, 20, 0)
            ffn_up_chunk(ti, W10, bW10, 21, 1)
            if ti + 1 < ntile:
                merge(gd, norm1_stream(ti + 1), ratio=[5, 1])
            else:
                run(gd)
        fin = mk("fin")
        P.op("sp", lambda h: h.nop(), [b_ for l_ in by_blk for b_ in l_] + [bdbg], [fin])
        P.finalize()

        esems = {e: es.enter_context(nc.semaphore("sem_" + e)) for e in ENGS}
        dsems = [es.enter_context(nc.semaphore(f"dsem{i}")) for i in range(len(P.dma_sem_counts))]
        block = es.enter_context(nc.Block())

        @block.sync
        def _(h):
            P.emit("sp", h, esems, dsems)

        @block.scalar
        def _(h):
            P.emit("act", h, esems, dsems)

        @block.vector
        def _(h):
            P.emit("dve", h, esems, dsems)

        @block.gpsimd
        def _(h):
            P.emit("pool", h, esems, dsems)

        @block.tensor
        def _(h):
            P.emit("pe", h, esems, dsems)
    build_program.dbg_list = dbg_list
    build_program.stats = dict(sbuf_bytes=sb_total[0], n_ops={e: len(P.ops[e]) for e in ENGS}, n_dsems=len(P.dma_sem_counts))
    return nc


_CONST_KEYS = ("ident", "mask_st", "ones_bf", "blk64", "onesz", "scanmask", "dm")


def _in_maps(inputs, seq, n_cores):
    consts = _host_consts()
    x = np.asarray(inputs["x"], dtype=np.float32)
    shared = {}
    for k in ("norm1_g", "w_in", "hgrn_out_g", "q_norm_g", "k_norm_g", "attn_sinks", "w_branch_a", "w_branch_b",
              "w_out", "norm2_g", "w_up", "conv_w", "conv_b", "w_down"):
        a = np.asarray(inputs[k], dtype=np.float32)
        shared[k] = np.ascontiguousarray(a[0])
    shared["hgrn_lb_logits"] = np.ascontiguousarray(np.asarray(inputs["hgrn_lb_logits"], dtype=np.float32))
    for k in _CONST_KEYS:
        shared["c_" + k] = consts[k]
    maps = []
    for c in range(n_cores):
        m = dict(shared)
        m["x"] = np.ascontiguousarray(x[c, :seq])
        maps.append(m)
    return maps


def kernel(**inputs):
    nc = build_program(SEQ)
    maps = _in_maps(inputs, SEQ, N_CORES)
    res = run_bass_kernel_spmd(nc, maps, core_ids=list(range(N_CORES)))
    out = np.stack([np.asarray(r["y"], dtype=np.float32) for r in res.results], axis=0)
    return out
```

```python
from contextlib import ExitStack

import numpy as np
import ml_dtypes

import concourse.bass as bass
import concourse.mybir as mybir
from concourse.bass_utils import run_bass_kernel_spmd

F32 = mybir.dt.float32
BF16 = mybir.dt.bfloat16
AF = mybir.ActivationFunctionType
ALU = mybir.AluOpType

N_CORES = 8
D = 1024
SEQ = 8192
T = 512
NB = T // 128
DFF = 2816
NFC = DFF // 128
EPS = 1e-6
NSLOT = 5
SAME_ENGINE_DIST = 4


def _needs_sync(p, o):
    if p.eng != o.eng:
        return True
    if o.eng in ("pe", "sp"):
        return False
    return (o.idx - p.idx) <= SAME_ENGINE_DIST

ENGS = ("pe", "act", "dve", "pool", "sp")


class Buf:
    __slots__ = ("name", "w", "r", "dsem", "dcount", "alias")

    def __init__(self, name):
        self.name = name
        self.w = None
        self.r = []
        self.dsem = None
        self.dcount = 0
        self.alias = []


class Op:
    __slots__ = ("eng", "fn", "deps", "signals", "sigval", "idx", "dma_ev")

    def __init__(self, eng, fn):
        self.eng = eng
        self.fn = fn
        self.deps = []
        self.signals = False
        self.sigval = 0
        self.idx = 0
        self.dma_ev = None


class Prog:
    def __init__(self):
        self.ops = {e: [] for e in ENGS}
        self.dma_sem_counts = []

    def _deps_for(self, reads, writes):
        deps = []
        for b in reads:
            if b.w is not None:
                deps.append(b.w)
        for b in writes:
            if b.w is not None:
                deps.append(b.w)
            deps.extend(b.r)
            for a in b.alias:
                if a.w is not None:
                    deps.append(a.w)
                deps.extend(a.r)
        return deps

    def _commit(self, ev, reads, writes):
        for b in reads:
            b.r.append(ev)
        for b in writes:
            b.w = ev
            b.r = []

    def op(self, eng, fn, reads=(), writes=()):
        o = Op(eng, fn)
        o.deps = self._deps_for(reads, writes)
        o.idx = len(self.ops[eng])
        self.ops[eng].append(o)
        self._commit(("op", o), reads, writes)
        return o

    def dma(self, fn, reads=(), writes=(), eng="sp", sem_buf=None):
        o = Op(eng, fn)
        b = sem_buf or (writes[0] if writes else reads[0])
        if b.dsem is None:
            b.dsem = len(self.dma_sem_counts)
            self.dma_sem_counts.append(0)
        deps = []
        for rb in reads:
            if rb.w is not None:
                deps.append(rb.w)
        for wb in writes:
            if wb.w is not None and not (wb.w[0] == "dma" and wb.w[1] == b.dsem):
                deps.append(wb.w)
            deps.extend(wb.r)
            for a in wb.alias:
                if a.w is not None:
                    deps.append(a.w)
                deps.extend(a.r)
        o.deps = deps
        o.idx = len(self.ops[eng])
        self.ops[eng].append(o)
        self.dma_sem_counts[b.dsem] += 16
        ev = ("dma", b.dsem, self.dma_sem_counts[b.dsem])
        o.dma_ev = ev
        self._commit(ev, reads, writes)
        return o

    def fence(self, engs=("pe", "act", "dve", "pool")):
        fb = [Buf("fence_" + e) for e in engs]
        for e, b in zip(engs, fb):
            self.op(e, lambda h: h.drain(), [], [b])
        for e in engs:
            self.op(e, lambda h: h.nop(), fb, [])

    def finalize(self):
        for e in ENGS:
            for o in self.ops[e]:
                for d in o.deps:
                    if d[0] == "op":
                        p = d[1]
                        if _needs_sync(p, o):
                            p.signals = True
        for e in ENGS:
            c = 0
            for o in self.ops[e]:
                if o.signals:
                    c += 1
                    o.sigval = c

    def emit(self, eng, h, esems, dsems):
        seen = {}
        for o in self.ops[eng]:
            need = {}
            for d in o.deps:
                if d[0] == "op":
                    p = d[1]
                    if not _needs_sync(p, o):
                        continue
                    key = ("e", p.eng)
                    val = p.sigval
                else:
                    key = ("d", d[1])
                    val = d[2]
                if val > need.get(key, 0):
                    need[key] = val
            for key, val in need.items():
                if seen.get(key, 0) >= val:
                    continue
                seen[key] = val
                sem = esems[key[1]] if key[0] == "e" else dsems[key[1]]
                h.wait_ge(sem, val)
            inst = o.fn(h)
            if o.dma_ev is not None:
                inst.then_inc(dsems[o.dma_ev[1]], 16)
            elif o.signals:
                inst.then_inc(esems[eng], 1)


def _host_consts():
    c = {}
    c["ident"] = np.eye(128, dtype=np.float32).astype(ml_dtypes.bfloat16)
    s = np.arange(128)[:, None]
    t = np.arange(128)[None, :]
    c["mask_st"] = (s <= t).astype(np.float32)
    c["ones_bf"] = np.ones((128, 128), dtype=np.float32).astype(ml_dtypes.bfloat16)
    blk = np.zeros((128, 128), dtype=np.float32)
    blk[:64, :64] = 1.0
    blk[64:, 64:] = 1.0
    c["blk64"] = blk.astype(ml_dtypes.bfloat16)
    oz = np.zeros((128, 2, 128), dtype=np.float32)
    oz[:, 0, :64] = 1.0
    oz[:, 1, 64:] = 1.0
    c["onesz"] = oz.astype(ml_dtypes.bfloat16)
    sm = np.ones((128, T), dtype=np.float32)
    sm[:, 0::128] = 0.0
    c["scanmask"] = sm
    dm = np.zeros((128, 2, 2, 4, 128), dtype=np.float64)
    for kv in range(2):
        for g in range(4):
            hh = kv * 4 + g
            slope = 2.0 ** (-8.0 * (hh + 1) / 8.0)
            d_cur = (t - s).astype(np.float64)
            d_prev = d_cur + 128.0
            dm[:, kv, 1, g, :] = np.where(d_cur >= 0, -slope * d_cur, -30000.0)
            dm[:, kv, 0, g, :] = np.where(d_prev < 128, -slope * d_prev, -30000.0)
    c["dm"] = dm.astype(np.float32)
    return c


(P_Q, P_G, P_F, P_I, P_AQ, P_AKV, P_GAB0, P_GAB1, P_GAB2, P_GAB3, P_WAB0, P_WAB1, P_WO0) = range(13)
P_GAB = (P_GAB0, P_GAB1, P_GAB2, P_GAB3)
P_WAB = (P_WAB0, P_WAB1)
P_UP0 = P_WO0 + 2
P_DN0 = P_UP0 + 11
NPIECE = P_DN0 + 6


def build_program(seq=SEQ, debug=False):
    ntile = seq // T
    nc = bass.Bass("TRN2", target_bir_lowering=False)
    dt_in = lambda name, shape, dt=F32: nc.dram_tensor(name, list(shape), dt, kind="ExternalInput").ap()
    x_d = dt_in("x", [seq, D])
    norm1_g = dt_in("norm1_g", [D])
    w_in = dt_in("w_in", [D, 4864])
    lb_logits = dt_in("hgrn_lb_logits", [2, 512])
    out_g = dt_in("hgrn_out_g", [128])
    qn_g = dt_in("q_norm_g", [64])
    kn_g = dt_in("k_norm_g", [64])
    sinks = dt_in("attn_sinks", [8])
    w_a = dt_in("w_branch_a", [512, D])
    w_b = dt_in("w_branch_b", [512, D])
    w_out = dt_in("w_out", [D, D])
    norm2_g = dt_in("norm2_g", [D])
    w_up = dt_in("w_up", [D, 2 * DFF])
    conv_w = dt_in("conv_w", [3, DFF])
    conv_b = dt_in("conv_b", [DFF])
    w_down = dt_in("w_down", [DFF, D])
    c_ident = dt_in("c_ident", [128, 128], BF16)
    c_mask = dt_in("c_mask_st", [128, 128])
    c_ones = dt_in("c_ones_bf", [128, 128], BF16)
    c_blk = dt_in("c_blk64", [128, 128], BF16)
    c_onesz = dt_in("c_onesz", [128, 2, 128], BF16)
    c_scan = dt_in("c_scanmask", [128, T])
    c_dm = dt_in("c_dm", [128, 2, 2, 4, 128])
    y_d = nc.dram_tensor("y", [seq, D], F32, kind="ExternalOutput").ap()
    wscr = nc.dram_tensor("wscr", [NPIECE, 128, 4096], BF16, kind="Internal").ap()
    dbg = {}
    if debug:
        dbg["x1"] = nc.dram_tensor("dbg_x1", [seq, D], F32, kind="ExternalOutput").ap()

    P = Prog()
    with ExitStack() as es:
        sb_total = [0]

        def sb(name, shape, dt):
            n = 1
            for s_ in shape[1:]:
                n *= s_
            sb_total[0] += n * (4 if dt == F32 else 2)
            return es.enter_context(nc.sbuf_tensor(name, list(shape), dt))

        X = [sb(f"X{i}", [128, NB, D], F32) for i in range(2)]
        hT = sb("hT", [128, 8, T], BF16)
        xs = [sb(f"xs{i}", [128, D], BF16) for i in range(2)]
        wslot = [sb(f"wslot{i}", [128, 4096], BF16) for i in range(NSLOT)]
        ident = sb("ident", [128, 128], BF16)
        mask_st = sb("mask_st", [128, 128], F32)
        ones_bf = sb("ones_bf", [128, 128], BF16)
        blk64 = sb("blk64", [128, 128], BF16)
        onesz = sb("onesz", [128, 2, 128], BF16)
        scanmask = sb("scanmask", [128, T], F32)
        dm = sb("dm", [128, 2, 2, 512], F32)
        g1 = sb("g1", [128, 8], F32)
        g2 = sb("g2", [128, 8], F32)
        lbl = sb("lbl", [128, 2, 4], F32)
        hco = sb("hco", [128, 4], F32)
        nhco = sb("nhco", [128, 4], F32)
        og = sb("og", [128, 1], F32)
        qg = sb("qg", [128, 1], F32)
        kg = sb("kg", [128, 1], F32)
        gqrow = sb("gqrow", [128, 64], F32)
        gkrow = sb("gkrow", [128, 64], F32)
        mq = sb("mq", [128, 4], F32)
        negshift = sb("negshift", [128, 1], F32)
        sinkraw = sb("sinkraw", [128, 4], F32)
        sinkexp = sb("sinkexp", [128, 4], F32)
        cw = sb("cw", [128, 3, NFC], F32)
        cb = sb("cb", [128, NFC], F32)
        epsc = sb("epsc", [128, 1], F32)
        mhalf = sb("mhalf", [128, 4], F32)
        ss = sb("ss", [128, NB], F32)
        ms = sb("ms", [128, NB], F32)
        rstd = sb("rstd", [128, NB], F32)
        nref = sb("nref", [128, 4, NB], F32)
        s1 = sb("s1", [128, 4, NB], F32)
        ddif = sb("ddif", [128, 4, NB], F32)
        s2 = sb("s2", [128, 4, NB], F32)
        dec = sb("dec", [128, 4, NB], F32)
        f32t = [sb(f"f32t{i}", [128, T], F32) for i in range(8)]
        tht = [f32t[0], f32t[1]]
        logf = [f32t[2], f32t[3]]
        cum = [f32t[4], f32t[5]]
        e1 = [f32t[6], f32t[7]]
        e3 = [f32t[2], f32t[3]]
        Ef = [f32t[2], f32t[3]]
        t1 = [f32t[4], f32t[5]]
        lnt = [f32t[0], f32t[1]]
        rden = f32t[7]
        bf32t = [Buf(f"f32t{i}") for i in range(8)]
        btht = [bf32t[0], bf32t[1]]
        blogf = [bf32t[2], bf32t[3]]
        bcum = [bf32t[4], bf32t[5]]
        be1 = [bf32t[6], bf32t[7]]
        be3 = [bf32t[2], bf32t[3]]
        bEf = [bf32t[2], bf32t[3]]
        bt1 = [bf32t[4], bf32t[5]]
        blnt = [bf32t[0], bf32t[1]]
        brden = bf32t[7]
        junk = f32t[1][:, :].bitcast(BF16)
        bjunk = bf32t[1]
        ARENA = 19472
        arena = sb("arena", [128, ARENA], BF16)

        def aview(off, n, dt, pat=None, **kw):
            v = arena[:, off:off + n]
            if dt == F32:
                v = v.bitcast(F32)
            if pat:
                v = v.rearrange(pat, **kw)
            return v

        O_VTM, O_QI, O_QD, O_KD, O_KI, O_ST, O_QS, O_GS, O_KF = 0, 2048, 4096, 6144, 8192, 10240, 11264, 13312, 15360
        qi = aview(O_QI, 2048, BF16, "p (h t) -> p h t", t=T)
        qd = aview(O_QD, 2048, BF16, "p (h t) -> p h t", t=T)
        kd = aview(O_KD, 2048, BF16, "p (h t) -> p h t", t=T)
        ki = aview(O_KI, 2048, BF16, "p (h t) -> p h t", t=T)
        qs = aview(O_QS, 2048, BF16, "p (h t) -> p h t", t=T)
        kf = aview(O_KF, 4096, F32, "p (h t) -> p h t", t=T)
        gs = aview(O_GS, 2048, BF16, "p (h t) -> p h t", t=T)
        vtm = aview(O_VTM, 2048, BF16, "p (b v) -> p b v", v=512)
        sT = [aview(O_ST + i * 512, 512, BF16, "p (h t) -> p h t", t=128) for i in range(2)]
        assert O_KF + 4096 <= ARENA
        act = aview(0, NFC * T, BF16, "p (c t) -> p c t", t=T)
        G = [aview(NFC * T + i * 2056, 2056, F32) for i in range(2)]
        c1 = [aview(NFC * T + 4112 + i * 1024, 1024, F32) for i in range(2)]
        c2 = [aview(NFC * T + 4112 + 2048 + i * 1024, 1024, F32) for i in range(2)]
        assert NFC * T + 4112 + 4096 <= ARENA

        kitm = [sb(f"kitm{i}", [128, 4, 128], BF16) for i in range(2)]
        S32 = sb("S32", [128, 4, 128], F32)
        Sb = sb("Sb", [128, 4, 128], BF16)
        sqo = [sb(f"sqo{i}", [128, 512], BF16) for i in range(2)]
        AT = sb("AT", [128, 4, T], BF16)
        qn = sb("qn", [128, 4, T], BF16)
        knz = [sb(f"knz{i}", [128, NB + 1, 128], BF16) for i in range(2)]
        vtz = [sb(f"vtz{i}", [128, NB + 1, 128], BF16) for i in range(2)]
        Ep = [sb(f"Ep{i}", [128, 512], BF16) for i in range(4)]
        BT = sb("BT", [128, 4, T], BF16)
        m1 = [sb(f"m1_{i}", [128, T], BF16) for i in range(2)]
        m2 = [sb(f"m2_{i}", [128, T], BF16) for i in range(2)]
        mixT = sb("mixT", [128, 8, T], BF16)
        gbuf = sb("gbuf", [128, 8, T], BF16)
        halo = sb("halo", [128, NFC, 2], F32)
        stage = [X[1][:, 0:2, :].rearrange("p b d -> p (b d)"), X[1][:, 2:4, :].rearrange("p b d -> p (b d)")]

        banks = [es.enter_context(nc.psum_tensor(f"bank{i}", [128, 512], F32)) for i in range(8)]

        def mk(name):
            return Buf(name)

        bX = [[mk(f"X{i}b{b}") for b in range(NB)] for i in range(2)]
        bhT = [mk(f"hT{b}") for b in range(NB)]
        bxs = [mk("xs0"), mk("xs1")]
        bslot = [mk(f"slot{i}") for i in range(NSLOT)]
        bconst = mk("const")
        bwscr = mk("wscr")
        bss, bms, brstd = mk("ss"), mk("ms"), mk("rstd")
        bqs = [mk(f"qs{h}") for h in range(4)]
        bkf = [mk(f"kf{h}") for h in range(4)]
        bgs = [mk(f"gs{h}") for h in range(4)]
        bvtm = [mk(f"vtm{b}") for b in range(NB)]
        bsm = [mk(f"small{h}") for h in range(4)]
        bqi = [mk(f"qi{h}") for h in range(4)]
        bqd = [mk(f"qd{h}") for h in range(4)]
        bkd = [mk(f"kd{h}") for h in range(4)]
        bki = [mk(f"ki{h}") for h in range(4)]
        bsT = [mk("sT0"), mk("sT1")]
        bkitm = [mk("kitm0"), mk("kitm1")]
        bS32 = [mk(f"S32_{h}") for h in range(4)]
        bSb = [mk(f"Sb{h}") for h in range(4)]
        bsqo = [mk("sqo0"), mk("sqo1")]
        bAT = [mk(f"AT{b}") for b in range(NB)]
        bqn = [mk(f"qn{j}") for j in range(4)]
        bkn = [mk(f"kn{b}") for b in range(NB + 1)]
        bvt = [mk(f"vt{b}") for b in range(NB + 1)]
        bEp = [mk(f"Ep{i}") for i in range(4)]
        bBT = [mk(f"BT{b}") for b in range(NB)]
        bm1 = [mk("m1_0"), mk("m1_1")]
        bm2 = [mk("m2_0"), mk("m2_1")]
        bmix = [mk(f"mix{c}") for c in range(8)]
        bgb = [mk(f"gb{c}") for c in range(8)]
        bG = [mk("G0"), mk("G1")]
        bc1 = [mk("c1_0"), mk("c1_1")]
        bc2 = [mk("c2_0"), mk("c2_1")]
        bhalo = [mk(f"halo{c}") for c in range(NFC)]
        bact = [mk(f"act{c}") for c in range(NFC)]
        bstage = [[bX[1][0], bX[1][1]], [bX[1][2], bX[1][3]]]
        bbank = [mk(f"bank{i}") for i in range(8)]
        by = mk("y")
        bdbg = mk("dbg")

        arena_map = []
        for h in range(4):
            arena_map += [(bqi[h], O_QI + h * 512, 512), (bqd[h], O_QD + h * 512, 512), (bkd[h], O_KD + h * 512, 512),
                          (bki[h], O_KI + h * 512, 512), (bqs[h], O_QS + h * 512, 512), (bkf[h], O_KF + h * 1024, 1024),
                          (bgs[h], O_GS + h * 512, 512)]
        for b in range(NB):
            arena_map.append((bvtm[b], O_VTM + b * 512, 512))
        for i in range(2):
            arena_map += [(bsT[i], O_ST + i * 512, 512), (bG[i], NFC * T + i * 2056, 2056),
                          (bc1[i], NFC * T + 4112 + i * 1024, 1024), (bc2[i], NFC * T + 4112 + 2048 + i * 1024, 1024)]
        for c in range(NFC):
            arena_map.append((bact[c], c * T, T))
        bstage23 = [mk("stage2"), mk("stage3")]
        arena_map += [(bstage23[0], 0, 4096), (bstage23[1], 4096, 4096)]
        for i_, (b1, o1, n1) in enumerate(arena_map):
            for (b2, o2, n2) in arena_map[i_ + 1:]:
                if o1 < o2 + n2 and o2 < o1 + n1:
                    b1.alias.append(b2)
                    b2.alias.append(b1)

        bank_ctr = [0]

        bank_reserved = set()

        def nextbank(reserve=False):
            while True:
                i = bank_ctr[0] % 8
                bank_ctr[0] += 1
                if i not in bank_reserved:
                    break
            if reserve:
                bank_reserved.add(i)
            return banks[i], bbank[i]

        def release_bank(bk):
            bank_reserved.discard(banks.index(bk))

        def A(out, in_, func, reads, writes, scale=1.0, bias=0.0, accum=None):
            if accum is None:
                P.op("act", lambda h: h.activation(out=out, in_=in_, func=func, scale=scale, bias=bias), reads, writes)
            else:
                P.op("act", lambda h: h.activation(out=out, in_=in_, func=func, scale=scale, bias=bias, accum_out=accum), reads, writes)

        def TS(eng, out, in0, s1_, s2_, op0, op1, reads, writes):
            if s2_ is None:
                P.op(eng, lambda h: h.tensor_scalar(out=out, in0=in0, scalar1=s1_, scalar2=None, op0=op0), reads, writes)
            else:
                P.op(eng, lambda h: h.tensor_scalar(out=out, in0=in0, scalar1=s1_, scalar2=s2_, op0=op0, op1=op1), reads, writes)

        def STT(out, in0, scalar, in1, op0, op1, reads, writes):
            P.op("dve", lambda h: h.scalar_tensor_tensor(out=out, in0=in0, scalar=scalar, in1=in1, op0=op0, op1=op1), reads, writes)

        def TT(eng, out, in0, in1, op, reads, writes):
            P.op(eng, lambda h: h.tensor_tensor(out=out, in0=in0, in1=in1, op=op), reads, writes)

        def CP(eng, out, in_, reads, writes):
            if eng == "act":
                P.op("act", lambda h: h.activation(out=out, in_=in_, func=AF.Copy), reads, writes)
            else:
                P.op(eng, lambda h: h.tensor_copy(out=out, in_=in_), reads, writes)

        def MM(out, lhsT, rhs, start, stop, reads, writes):
            P.op("pe", lambda h: h.matmul(out, lhsT=lhsT, rhs=rhs, start=start, stop=stop), reads, writes)

        def TR(out, in_, reads, writes):
            P.op("pe", lambda h: h.transpose(out=out, in_=in_, identity=ident[:]), list(reads) + [bconst], writes)

        def DMA(out, in_, reads, writes, sem_buf=None, slow=False):
            if slow:
                P.dma(lambda h: h.dma_start(out=out, in_=in_, allow_slow_non_contiguous=True), reads, writes, sem_buf=sem_buf)
            else:
                P.dma(lambda h: h.dma_start(out=out, in_=in_), reads, writes, sem_buf=sem_buf)

        dbg_list = []

        def DBG(name, ap, reads, dt=F32):
            if not debug:
                return
            shape = list(ap.shape)
            d_ = nc.dram_tensor("dbg_" + name, shape, dt, kind="ExternalOutput").ap()
            DMA(d_, ap, reads, [bdbg])
            dbg_list.append(name)

        DMA(ident[:], c_ident, [], [bconst])
        DMA(mask_st[:], c_mask, [], [bconst])
        DMA(ones_bf[:], c_ones, [], [bconst])
        DMA(blk64[:], c_blk, [], [bconst])
        DMA(onesz[:], c_onesz, [], [bconst])
        DMA(scanmask[:], c_scan, [], [bconst])
        DMA(dm[:], c_dm.rearrange("p a b g t -> p a b (g t)"), [], [bconst])
        DMA(g1[:], norm1_g.rearrange("(c p) -> p c", p=128), [], [bconst], slow=True)
        DMA(g2[:], norm2_g.rearrange("(c p) -> p c", p=128), [], [bconst], slow=True)
        DMA(lbl[:], lb_logits.rearrange("j (h p) -> p j h", p=128), [], [bconst], slow=True)
        DMA(og[:], out_g.rearrange("(p o) -> p o", o=1), [], [bconst], slow=True)
        for half in range(2):
            DMA(qg[half * 64:(half + 1) * 64, :], qn_g.rearrange("(p o) -> p o", o=1), [], [bconst], slow=True)
            DMA(kg[half * 64:(half + 1) * 64, :], kn_g.rearrange("(p o) -> p o", o=1), [], [bconst], slow=True)
            DMA(sinkraw[half * 64:(half + 1) * 64, :],
                sinks[half * 4:(half + 1) * 4].rearrange("(o g) -> o g", o=1).broadcast_to([64, 4]), [], [bconst], slow=True)
        DMA(gqrow[:], qn_g.rearrange("(o g) -> o g", o=1).broadcast_to([128, 64]), [], [bconst], slow=True)
        DMA(gkrow[:], kn_g.rearrange("(o g) -> o g", o=1).broadcast_to([128, 64]), [], [bconst], slow=True)
        DMA(cw[:], conv_w.rearrange("k (c p) -> p k c", p=128), [], [bconst], slow=True)
        DMA(cb[:], conv_b.rearrange("(c p) -> p c", p=128), [], [bconst], slow=True)

        bprm = mk("params")
        TT("dve", hco[:], lbl[:, 0, :], lbl[:, 1, :], ALU.subtract, [bconst], [bprm])
        A(hco[:], hco[:], AF.Tanh, [bprm], [bprm], scale=0.5)
        TS("dve", nhco[:], hco[:], 0.25, -0.25, ALU.mult, ALU.add, [bprm], [bprm])
        TS("dve", hco[:], nhco[:], -1.0, None, ALU.mult, None, [bprm], [bprm])
        TS("dve", og[:], og[:], 0.5, None, ALU.mult, None, [bconst], [bprm])
        TS("dve", qg[:], qg[:], 0.125, None, ALU.mult, None, [bconst], [bprm])
        for i_, grow in enumerate((gqrow, gkrow)):
            P.op("dve", lambda h, i_=i_, grow=grow: h.tensor_reduce(out=mq[:, i_:i_ + 1], in_=grow[:], op=ALU.max, axis=mybir.AxisListType.X), [bconst], [bprm])
            P.op("dve", lambda h, i_=i_, grow=grow: h.tensor_reduce(out=mq[:, 2 + i_:3 + i_], in_=grow[:], op=ALU.min, axis=mybir.AxisListType.X), [bconst], [bprm])
            TS("dve", mq[:, 2 + i_:3 + i_], mq[:, 2 + i_:3 + i_], -1.0, None, ALU.mult, None, [bprm], [bprm])
            TT("dve", mq[:, i_:i_ + 1], mq[:, i_:i_ + 1], mq[:, 2 + i_:3 + i_], ALU.max, [bprm], [bprm])
        STT(negshift[:], mq[:, 0:1], -8.0, mq[:, 1:2], ALU.mult, ALU.mult, [bprm], [bprm])
        A(sinkexp[:], sinkraw[:], AF.Exp, [bconst, bprm], [bprm], bias=negshift[:, 0:1])
        P.op("pool", lambda h: h.memset(epsc[:], EPS), [], [bprm])
        P.op("pool", lambda h: h.memset(mhalf[:], -0.5), [], [bprm])
        P.op("pool", lambda h: h.memset(S32[:], 0.0), [], bS32)
        P.op("pool", lambda h: h.memset(Sb[:], 0.0), [], bSb)
        P.op("pool", lambda h: h.memset(halo[:], 0.0), [], bhalo)
        for i_ in range(2):
            P.op("pool", lambda h, i_=i_: h.memset(knz[i_][:], 0.0), [], bkn)
            P.op("pool", lambda h, i_=i_: h.memset(vtz[i_][:], 0.0), [], bvt)

        cast_ctr = [0]
        bwp = [[mk(f"wscr{i}_{h}") for h in range(2)] for i in range(NPIECE)]
        stage4 = stage + [arena[:, 0:4096].bitcast(F32), arena[:, 4096:8192].bitcast(F32)]
        bstage4 = bstage + [[bstage23[0]], [bstage23[1]]]

        cast_jobs = []
        bslq = [[mk(f"slot{i}q{q}") for q in range(4)] for i in range(NSLOT)]

        def cast_half(pi, half, load_list, scale=None, perm=None):
            cast_jobs.append((pi, half, load_list, scale, perm))

        def cast_loads(k):
            pi, half, load_list, scale, perm = cast_jobs[k]
            st, bst = stage4[k % 4], bstage4[k % 4]
            st3 = st.rearrange("p (q n) -> p q n", q=4)
            for dstf, src in load_list:
                DMA(dstf(st3), src, [], bst, sem_buf=bst[0], slow=True)

        def cast_compute(k):
            pi, half, load_list, scale, perm = cast_jobs[k]
            st, bst = stage4[k % 4], bstage4[k % 4]
            sidx = k % NSLOT
            dst = wslot[sidx][:, 0:2048]
            for q in range(4):
                kc = half * 4 + q
                o_ap = dst[:, q * 512:(q + 1) * 512]
                i_ap = st[:, q * 512:(q + 1) * 512]
                if perm is not None:
                    o_ap, i_ap = perm(o_ap, i_ap)
                wr = [bslq[sidx][q]]
                if scale is not None and q % 2 == 0:
                    A(o_ap, i_ap, AF.Copy, bst + [bconst], wr, scale=scale[:, kc:kc + 1])
                elif scale is not None:
                    TS("dve", o_ap, i_ap, scale[:, kc:kc + 1], None, ALU.mult, None, bst + [bconst], wr)
                elif (k + q) % 2 == 0:
                    CP("dve", o_ap, i_ap, bst, wr)
                else:
                    CP("act", o_ap, i_ap, bst, wr)

        def cast_store(k):
            pi, half = cast_jobs[k][0], cast_jobs[k][1]
            sidx = k % NSLOT
            DMA(wscr[pi, :, half * 2048:(half + 1) * 2048], wslot[sidx][:, 0:2048], [bslot[sidx]] + bslq[sidx], [bwp[pi][half]], sem_buf=bslot[sidx])

        def run_cast_jobs(ahead=3):
            n = len(cast_jobs)
            for k in range(min(ahead, n)):
                cast_loads(k)
            for k in range(n):
                cast_compute(k)
                if k + ahead < n:
                    cast_loads(k + ahead)
                cast_store(k)

        def rows4(w, r0, c0, ncols, nq=4):
            return w[r0:r0 + nq * 128, c0:c0 + ncols].rearrange("(q p) n -> p q n", p=128)

        def std_piece(pi, w, c0, ncols, scale=None):
            for half in range(2):
                cast_half(pi, half, [(lambda st3: st3[:, :, 0:ncols], rows4(w, half * 512, c0, ncols))], scale)

        HG_Q0, HG_F0, HG_I0, HG_G0, AT_Q0, AT_K0, AT_V0, GA0, GB0 = 0, 512, 1024, 1536, 2048, 2560, 2688, 2816, 3840
        std_piece(P_Q, w_in, HG_Q0, 512, g1)
        std_piece(P_G, w_in, HG_G0, 512, g1)
        std_piece(P_F, w_in, HG_F0, 512, g1)
        std_piece(P_I, w_in, HG_I0, 512, g1)
        perm_q = lambda o_ap, i_ap: (o_ap.rearrange("p (j kv d) -> p j kv d", j=4, kv=2, d=64),
                                     i_ap.rearrange("p (kv j d) -> p j kv d", kv=2, j=4, d=64))
        for half in range(2):
            cast_half(P_AQ, half, [(lambda st3: st3[:, :, :], rows4(w_in, half * 512, AT_Q0, 512))], g1, perm=perm_q)
        std_piece(P_AKV, w_in, AT_K0, 256, g1)
        for hf in range(2):
            cast_half(P_WAB[hf], 0, [(lambda st3: st3[:, :, :], rows4(w_a, 0, hf * 512, 512))])
            ll = []
            for g in range(4):
                for kv in range(2):
                    r0 = (kv * 4 + g) * 64
                    ll.append((lambda st3, g=g, kv=kv: st3[kv * 64:(kv + 1) * 64, g, :], w_b[r0:r0 + 64, hf * 512:(hf + 1) * 512]))
            cast_half(P_WAB[hf], 1, ll)
        perm_pair = lambda o_ap, i_ap: (o_ap.rearrange("p (cc xy e) -> p cc xy e", cc=2, xy=2, e=128),
                                        i_ap.rearrange("p (xy cc e) -> p cc xy e", xy=2, cc=2, e=128))
        for k4 in range(4):
            for half in range(2):
                cast_half(P_GAB[k4], half,
                          [(lambda st3: st3[:, :, 0:256], rows4(w_in, half * 512, GA0 + k4 * 256, 256)),
                           (lambda st3: st3[:, :, 256:512], rows4(w_in, half * 512, GB0 + k4 * 256, 256))], g1, perm=perm_pair)
        std_piece(P_WO0, w_out, 0, 512)
        std_piece(P_WO0 + 1, w_out, 512, 512)
        for p in range(11):
            for half in range(2):
                cast_half(P_UP0 + p, half,
                          [(lambda st3: st3[:, :, 0:256], rows4(w_up, half * 512, p * 256, 256)),
                           (lambda st3: st3[:, :, 256:512], rows4(w_up, half * 512, DFF + p * 256, 256))], g2, perm=perm_pair)
        for hf in range(2):
            for kgp in range(3):
                pi = P_DN0 + hf * 3 + kgp
                nk = 8 if kgp < 2 else NFC - 16
                for half in range(2):
                    nq = min(4, nk - half * 4)
                    if nq > 0:
                        cast_half(pi, half, [(lambda st3, nq=nq: st3[:, 0:nq, :], rows4(w_down, (kgp * 8 + half * 4) * 128, hf * 512, 512, nq))])

        run_cast_jobs()

        stream_issued = [0]
        total_stream = ntile * NPIECE

        def issue_loads_upto(gidx):
            while stream_issued[0] <= min(gidx, total_stream - 1):
                k = stream_issued[0]
                DMA(wslot[k % NSLOT][:, :], wscr[k % NPIECE], bwp[k % NPIECE], [bslot[k % NSLOT]])
                stream_issued[0] += 1

        last_piece = [-1]

        def piece(ti, pi, hold=None):
            gidx = ti * NPIECE + pi
            assert gidx >= last_piece[0]
            last_piece[0] = gidx
            oldest = gidx if hold is None else ti * NPIECE + hold
            assert gidx <= oldest + NSLOT - 1
            issue_loads_upto(oldest + NSLOT - 1)
            return wslot[gidx % NSLOT], bslot[gidx % NSLOT]

        bss_b = [mk(f"ss{b}") for b in range(NB)]
        brstd_b = [mk(f"rstd{b}") for b in range(NB)]
        by_blk = [[mk(f"y{i}_{b}") for b in range(NB)] for i in range(2)]

        def load_x(ti):
            xb = ti % 2
            for b in range(NB):
                r0 = ti * T + b * 128
                DMA(X[xb][:, b, :], x_d[r0:r0 + 128, :], [], [bX[xb][b]])

        def norm_stats(xb, b):
            A(junk, X[xb][:, b, :], AF.Square, [bX[xb][b]], [bjunk, bss_b[b]], accum=ss[:, b:b + 1])
            TS("dve", ms[:, b:b + 1], ss[:, b:b + 1], 1.0 / D, EPS, ALU.mult, ALU.add, [bss_b[b]], [bss_b[b]])
            TT("pool", rstd[:, b:b + 1], ms[:, b:b + 1], mhalf[:, 0:1], ALU.pow, [bss_b[b], bprm], [brstd_b[b]])
            k = b % 2
            TS("dve", xs[k][:], X[xb][:, b, :], rstd[:, b:b + 1], None, ALU.mult, None, [bX[xb][b], brstd_b[b]], [bxs[k]])

        def norm_transpose(xb, b):
            k = b % 2
            for half in range(2):
                bk, bbk = nextbank()
                for c4 in range(4):
                    c = half * 4 + c4
                    MM(bk[:, c4 * 128:(c4 + 1) * 128], xs[k][:, c * 128:(c + 1) * 128], ident[:], True, True, [bxs[k], bconst], [bbk])
                CP("act", hT[:, half * 4:(half + 1) * 4, b * 128:(b + 1) * 128], bk[:].rearrange("p (c t) -> p c t", t=128), [bbk], [bhT[b]])

        def norm_block(xb, b):
            norm_stats(xb, b)
            norm_transpose(xb, b)

        def norm1_stream(ti):
            for b in range(NB):
                norm_block(ti % 2, b)
                yield

        def fm_group(W, bW, c0, ncol=128, kcs=8, wstride=512):
            bk, bbk = nextbank()
            for kc in range(kcs):
                MM(bk[0:ncol, :], W[:, kc * wstride + c0: kc * wstride + c0 + ncol], hT[:, kc, :], kc == 0, kc == kcs - 1,
                   [bW] + bhT, [bbk])
            return bk, bbk

        def merge(*gens, ratio=None, delay=None):
            gens = list(gens)
            ratio = ratio or [1] * len(gens)
            delay = list(delay or [0] * len(gens))
            alive = [True] * len(gens)
            while any(alive):
                for i, g in enumerate(gens):
                    if delay[i] > 0:
                        delay[i] -= 1
                        continue
                    for _ in range(ratio[i]):
                        if alive[i]:
                            try:
                                next(g)
                            except StopIteration:
                                alive[i] = False

        def run(g):
            for _ in g:
                pass

        def phase_tanh(ti):
            W, bW = piece(ti, P_Q)
            for h in range(4):
                bk, bbk = fm_group(W, bW, h * 128)
                k = h % 2
                A(tht[k][:], bk[:], AF.Tanh, [bbk], [btht[k]], scale=0.5)
                STT(qs[:, h, :], tht[k][:], 1.0, bk[:], ALU.add, ALU.mult, [btht[k], bbk], [bqs[h]])
            W, bW = piece(ti, P_G)
            for h in range(4):
                bk, bbk = fm_group(W, bW, h * 128)
                k = h % 2
                A(tht[k][:], bk[:], AF.Tanh, [bbk], [btht[k]], scale=0.5)
                STT(gs[:, h, :], tht[k][:], 1.0, bk[:], ALU.add, ALU.mult, [btht[k], bbk], [bgs[h]])
            W, bW = piece(ti, P_F)
            for h in range(4):
                bk, bbk = fm_group(W, bW, h * 128)
                k = h % 2
                A(tht[k][:], bk[:], AF.Tanh, [bbk], [btht[k]], scale=0.5)
                TS("dve", kf[:, h, :], tht[k][:], nhco[:, h:h + 1], hco[:, h:h + 1], ALU.mult, ALU.add, [btht[k], bprm], [bkf[h]])
            if ti == 0:
                DBG("qs", qs, bqs, BF16)
                DBG("kf", kf, bkf)
                DBG("gs", gs, bgs, BF16)

        def proj2_stream(ti):
            W, bW = piece(ti, P_I)
            for b in range(NB):
                bk, bbk = nextbank()
                for kc in range(8):
                    MM(bk[:], hT[:, kc, b * 128:(b + 1) * 128], W[:, kc * 512:(kc + 1) * 512], kc == 0, kc == 7, [bW, bhT[b]], [bbk])
                CP("act", vtm[:, b, :], bk[:], [bbk], [bvtm[b]])
                yield
            Wq, bWq = piece(ti, P_AQ)
            for j in range(5):
                k = j % 2
                if j < 4:
                    bk, bbk = fm_group(Wq, bWq, j * 128)
                else:
                    Wk, bWk = piece(ti, P_AKV)
                    bk, bbk = fm_group(Wk, bWk, 0)
                A(sqo[k][:], bk[:], AF.Square, [bbk], [bsqo[k]])
                bk2, bbk2 = nextbank()
                MM(bk2[:], blk64[:], sqo[k][:], True, True, [bconst, bsqo[k]], [bbk2])
                yield
                A(lnt[k][:], bk2[:], AF.Ln, [bbk2, bprm], [blnt[k]], scale=1.0 / 64, bias=epsc[:, 0:1])
                A(lnt[k][:], lnt[k][:], AF.Exp, [blnt[k]], [blnt[k]], scale=-0.5)
                if j < 4:
                    STT(qn[:, j, :], bk[:], qg[:, 0:1], lnt[k][:], ALU.mult, ALU.mult, [bbk, blnt[k], bprm], [bqn[j]])
                else:
                    for kv in range(2):
                        ps_ = slice(kv * 64, (kv + 1) * 64)
                        STT(knz[kv][ps_, 1:NB + 1, :], bk[ps_, :].rearrange("p (b t) -> p b t", t=128), kg[ps_, 0:1],
                            lnt[k][ps_, :].rearrange("p (b t) -> p b t", t=128), ALU.mult, ALU.mult, [bbk, blnt[k], bconst], bkn[1:])
                yield
            for b in range(NB):
                bk, bbk = nextbank()
                for kc in range(8):
                    MM(bk[:, 0:128], hT[:, kc, b * 128:(b + 1) * 128], Wk[:, kc * 512 + 128: kc * 512 + 256], kc == 0, kc == 7, [bWk, bhT[b]], [bbk])
                for kv in range(2):
                    CP("act", vtz[kv][:, b + 1, kv * 64:(kv + 1) * 64], bk[:, kv * 64:(kv + 1) * 64], [bbk], [bvt[b + 1]])
                if b % 2 == 1:
                    yield
            if ti == 0:
                DBG("vtm", vtm, bvtm, BF16)
                DBG("qn", qn[:], bqn, BF16)
                for kv in range(2):
                    DBG(f"knz{kv}", knz[kv][:], bkn, BF16)
                    DBG(f"vtz{kv}", vtz[kv][:], bvt, BF16)

        def algebra_stream(ti, heads=(0, 1, 2, 3)):
            for h in heads:
                k = h % 2
                A(logf[k][:], kf[:, h, :], AF.Ln, [bkf[h]], [blogf[k]], scale=-1.0, bias=1.0)
                P.op("dve", lambda hh, k=k: hh.tensor_tensor_scan(out=cum[k][:], data0=scanmask[:], data1=logf[k][:], initial=0.0,
                                                                   op0=ALU.mult, op1=ALU.add), [bconst, blogf[k]], [bcum[k]])
                yield
                cum3 = cum[k][:].rearrange("p (b t) -> p b t", t=128)
                TS("dve", nref[:, h, :], cum3[:, :, 63], -1.0, None, ALU.mult, None, [bcum[k]], [bsm[h]])
                TT("dve", ddif[:, h, :], cum3[:, :, 127], cum3[:, :, 63], ALU.subtract, [bcum[k]], [bsm[h]])
                A(e1[k][:], cum[k][:], AF.Exp, [bcum[k]], [be1[k]])
                A(s1[:, h, :], nref[:, h, :], AF.Exp, [bsm[h]], [bsm[h]])
                A(s2[:, h, :], ddif[:, h, :], AF.Exp, [bsm[h]], [bsm[h]])
                A(dec[:, h, :], cum3[:, :, 127], AF.Exp, [bcum[k]], [bsm[h]])
                STT(qi[:, h, :], qs[:, h, :], 0.5 * (128.0 ** -0.5), e1[k][:], ALU.mult, ALU.mult, [bqs[h], be1[k]], [bqi[h]])
                yield
                for b in range(NB):
                    sl = slice(b * 128, (b + 1) * 128)
                    A(e3[k][:, sl], cum[k][:, sl], AF.Exp, [bcum[k]], [be3[k]], scale=-1.0, bias=cum[k][:, b * 128 + 63: b * 128 + 64])
                for b in range(NB):
                    sl = slice(b * 128, (b + 1) * 128)
                    TS("dve", qd[:, h, sl], qi[:, h, sl], s1[:, h, b:b + 1], None, ALU.mult, None, [bqi[h], bsm[h]], [bqd[h]])
                yield
                TT("dve", kd[:, h, :], kf[:, h, :], e3[k][:], ALU.mult, [bkf[h], be3[k]], [bkd[h]])
                for b in range(NB):
                    sl = slice(b * 128, (b + 1) * 128)
                    TS("dve", ki[:, h, sl], kd[:, h, sl], s2[:, h, b:b + 1], None, ALU.mult, None, [bkd[h], bsm[h]], [bki[h]])
                yield
            if ti == 0 and heads[-1] == 3:
                DBG("qi", qi, bqi, BF16)
                DBG("qd", qd, bqd, BF16)
                DBG("kd", kd, bkd, BF16)
                DBG("ki", ki, bki, BF16)
                DBG("dec", dec[:], bsm)

        hg_bkU = [None] * NB

        def hgA_stream(ti, blocks=(0, 1, 2, 3)):
            for b in blocks:
                sl = slice(b * 128, (b + 1) * 128)
                k = b % 2
                bkS, bbkS = nextbank()
                for h in range(4):
                    MM(bkS[:, h * 128:(h + 1) * 128], kd[:, h, sl], qd[:, h, sl], True, True, [bkd[h], bqd[h]], [bbkS])
                bkT, bbkT = nextbank()
                pv = bkT[:].bitcast(BF16)
                for h in range(4):
                    TR(pv[:, h * 128:(h + 1) * 128], ki[:, h, sl], [bki[h]], [bbkT])
                yield
                for h in range(4):
                    TT("dve", sT[k][:, h, :], bkS[:, h * 128:(h + 1) * 128], mask_st[:], ALU.mult, [bbkS, bconst], [bsT[k]])
                CP("act", kitm[k][:], pv[:, 0:512].rearrange("p (h t) -> p h t", t=128), [bbkT], [bkitm[k]])
                bkU, bbkU = nextbank(reserve=True)
                for h in range(4):
                    MM(bkU[:, h * 128:(h + 1) * 128], kitm[k][:, h, :], vtm[:, b, h * 128:(h + 1) * 128], True, True, [bkitm[k], bvtm[b]], [bbkU])
                hg_bkU[b] = (bkU, bbkU)
                yield

        def hgB_stream(ti):
            for b in range(NB):
                sl = slice(b * 128, (b + 1) * 128)
                k = b % 2
                bkU, bbkU = hg_bkU[b]
                bkO, bbkO = nextbank()
                for h in range(4):
                    MM(bkO[:, h * 128:(h + 1) * 128], vtm[:, b, h * 128:(h + 1) * 128], sT[k][:, h, :], True, False, [bvtm[b], bsT[k]], [bbkO])
                    MM(bkO[:, h * 128:(h + 1) * 128], Sb[:, h, :], qi[:, h, sl], False, True, [bSb[h], bqi[h]], [bbkO])
                for h in range(4):
                    STT(S32[:, h, :], S32[:, h, :], dec[:, h, b:b + 1], bkU[:, h * 128:(h + 1) * 128], ALU.mult, ALU.add,
                        [bS32[h], bsm[h], bbkU], [bS32[h]])
                release_bank(bkU)
                CP("dve", Sb[:], S32[:], bS32, bSb)
                A(sqo[k][:], bkO[:], AF.Square, [bbkO], [bsqo[k]])
                bkN, bbkN = nextbank()
                MM(bkN[:], ones_bf[:], sqo[k][:], True, True, [bconst, bsqo[k]], [bbkN])
                yield
                A(lnt[k][:], bkN[:], AF.Ln, [bbkN, bprm], [blnt[k]], scale=1.0 / 128, bias=epsc[:, 0:1])
                A(lnt[k][:], lnt[k][:], AF.Exp, [blnt[k]], [blnt[k]], scale=-0.5)
                STT(t1[k][:], bkO[:], og[:, 0:1], lnt[k][:], ALU.mult, ALU.mult, [bbkO, blnt[k], bprm], [bt1[k]])
                TT("pool", AT[:, :, sl], t1[k][:].rearrange("p (h t) -> p h t", t=128), gs[:, :, sl], ALU.mult, [bt1[k]] + bgs, [bAT[b]])
                yield
            if ti == 0:
                DBG("AT", AT[:], bAT, BF16)

        def swa_block_stream(ti):
            for b in range(NB):
                sl = slice(b * 128, (b + 1) * 128)
                gb = ti * NB + b
                kbs = (1,) if gb == 0 else (0, 1)
                for kv in range(2):
                    for kb in kbs:
                        bkC, bbkC = nextbank()
                        MM(bkC[:].rearrange("p (g t) -> p g t", t=128), knz[kv][:, b + kb, :], qn[:, :, sl], True, True, [bkn[b + kb]] + bqn, [bbkC])
                        e = kb % 2
                        TT("dve", Ef[e][:], bkC[:], dm[:, kv, kb, :], ALU.add, [bbkC, bconst], [bEf[e]])
                        A(Ep[kv * 2 + kb][:], Ef[e][:], AF.Exp, [bEf[e], bprm], [bEp[kv * 2 + kb]], bias=negshift[:, 0:1])
                    yield
                bkP, bbkP = nextbank()
                bkD, bbkD = nextbank()
                pairs = [(kv, kb) for kv in range(2) for kb in kbs]
                for i_, (kv, kb) in enumerate(pairs):
                    MM(bkP[:], vtz[kv][:, b + kb, :], Ep[kv * 2 + kb][:], i_ == 0, i_ == len(pairs) - 1, [bvt[b + kb], bEp[kv * 2 + kb]], [bbkP])
                for i_, (kv, kb) in enumerate(pairs):
                    MM(bkD[:], onesz[:, kv, :], Ep[kv * 2 + kb][:], i_ == 0, i_ == len(pairs) - 1, [bconst, bEp[kv * 2 + kb]], [bbkD])
                yield
                for g in range(4):
                    gsl = slice(g * 128, (g + 1) * 128)
                    A(rden[:, gsl], bkD[:, gsl], AF.Ln, [bbkD, bprm], [brden], bias=sinkexp[:, g:g + 1])
                A(rden[:], rden[:], AF.Exp, [brden], [brden], scale=-1.0)
                TT("dve", BT[:, :, sl], bkP[:].rearrange("p (g t) -> p g t", t=128), rden[:].rearrange("p (g t) -> p g t", t=128),
                   ALU.mult, [bbkP, brden], [bBT[b]])
                yield
            if ti == 0:
                DBG("BT", BT[:], bBT, BF16)

        def gates_stream(ti):
            for k4 in range(4):
                Wg, bWg = piece(ti, P_GAB[k4])
                for cc in range(2):
                    dc = 2 * k4 + cc
                    bkGa, bbkGa = fm_group(Wg, bWg, (2 * cc) * 128)
                    CP("act", mixT[:, dc, :], bkGa[:], [bbkGa], [bmix[dc]])
                    yield
                    bkGb, bbkGb = fm_group(Wg, bWg, (2 * cc + 1) * 128)
                    CP("dve", gbuf[:, dc, :], bkGb[:], [bbkGb], [bgb[dc]])
                    yield

        def branches(ti):
            for hf in range(2):
                Wab, bWab = piece(ti, P_WAB[hf])
                for c4 in range(4):
                    dc = hf * 4 + c4
                    k = dc % 2
                    A(tht[0][:], mixT[:, dc, :], AF.Tanh, [bmix[dc]], [btht[0]], scale=0.5)
                    A(tht[1][:], gbuf[:, dc, :], AF.Tanh, [bgb[dc]], [btht[1]], scale=0.5)
                    bkA, bbkA = nextbank()
                    for h in range(4):
                        MM(bkA[:], Wab[:, h * 512 + c4 * 128: h * 512 + (c4 + 1) * 128], AT[:, h, :], h == 0, h == 3, [bWab] + bAT, [bbkA])
                    bkB, bbkB = nextbank()
                    for g in range(4):
                        MM(bkB[:], Wab[:, (4 + g) * 512 + c4 * 128: (4 + g) * 512 + (c4 + 1) * 128], BT[:, g, :], g == 0, g == 3, [bWab] + bBT, [bbkB])
                    STT(m1[k][:], tht[0][:], 1.0, bkA[:], ALU.add, ALU.mult, [btht[0], bbkA], [bm1[k]])
                    STT(m2[k][:], tht[1][:], 1.0, bkB[:], ALU.add, ALU.mult, [btht[1], bbkB], [bm2[k]])
                    TT("pool", mixT[:, dc, :], m1[k][:], m2[k][:], ALU.add, [bm1[k], bm2[k]], [bmix[dc]])
            if ti == 0:
                DBG("mixT", mixT[:], bmix, BF16)

        def wout_norm2(ti):
            xb = ti % 2
            W0, bW0 = piece(ti, P_WO0)
            W1, bW1 = piece(ti, P_WO0 + 1, hold=P_WO0)
            for b in range(NB):
                for hf, (W, bW) in enumerate(((W0, bW0), (W1, bW1))):
                    bk, bbk = nextbank()
                    for dc in range(8):
                        MM(bk[:], mixT[:, dc, b * 128:(b + 1) * 128], W[:, dc * 512:(dc + 1) * 512], dc == 0, dc == 7, [bW, bmix[dc]], [bbk])
                    xsl = X[xb][:, b, hf * 512:(hf + 1) * 512]
                    STT(xsl, bk[:], 0.5, xsl, ALU.mult, ALU.add, [bbk, bX[xb][b]], [bX[xb][b]])
                norm_stats(xb, b)
                if b >= 1:
                    norm_transpose(xb, b - 1)
            norm_transpose(xb, NB - 1)
            if debug:
                for b in range(NB):
                    r0 = ti * T + b * 128
                    DMA(dbg["x1"][r0:r0 + 128, :], X[xb][:, b, :], [bX[xb][b]], [bdbg])

        def ffn_up_chunk(ti, W, bW, c, cc):
            k = c % 2
            bkG, bbkG = fm_group(W, bW, (2 * cc) * 128)
            bkV, bbkV = fm_group(W, bW, (2 * cc + 1) * 128)
            CP("act", G[k][:, 2:T + 2], bkG[:], [bbkG], [bG[k]])
            CP("pool", G[k][:, 0:2], halo[:, c, :], [bhalo[c]], [bG[k]])
            TS("dve", c1[k], G[k][:, 0:T], cw[:, 0, c:c + 1], cb[:, c:c + 1], ALU.mult, ALU.add, [bG[k], bconst], [bc1[k]])
            STT(c2[k], G[k][:, 1:T + 1], cw[:, 1, c:c + 1], c1[k], ALU.mult, ALU.add, [bG[k], bc1[k], bconst], [bc2[k]])
            STT(c1[k], bkG[:], cw[:, 2, c:c + 1], c2[k], ALU.mult, ALU.add, [bbkG, bc2[k], bconst], [bc1[k]])
            CP("pool", halo[:, c, :], G[k][:, T:T + 2], [bG[k]], [bhalo[c]])
            A(c2[k], c1[k], AF.Gelu, [bc1[k]], [bc2[k]])
            TT("dve", act[:, c, :], c2[k], bkV[:], ALU.mult, [bc2[k], bbkV], [bact[c]])

        def ffn_up(ti, pieces):
            for p in pieces:
                W, bW = piece(ti, P_UP0 + p)
                for cc in range(2):
                    ffn_up_chunk(ti, W, bW, 2 * p + cc, cc)

        def ffn_down_stream(ti, hold_first=None):
            xb = ti % 2
            for hf in range(2):
                bks = [nextbank(reserve=True) for _ in range(NB)]
                for kgp in range(3):
                    W, bW = piece(ti, P_DN0 + hf * 3 + kgp, hold=hold_first if (hf == 0 and kgp == 0) else None)
                    nk = 8 if kgp < 2 else NFC - 16
                    for b in range(NB):
                        for kc in range(nk):
                            c = kgp * 8 + kc
                            MM(bks[b][0][:], act[:, c, b * 128:(b + 1) * 128], W[:, kc * 512:(kc + 1) * 512], c == 0, c == NFC - 1, [bW, bact[c]], [bks[b][1]])
                        if not (hf == 1 and kgp == 2):
                            yield
                for b in range(NB):
                    xsl = X[xb][:, b, hf * 512:(hf + 1) * 512]
                    TT("dve", xsl, bks[b][0][:], xsl, ALU.add, [bks[b][1], bX[xb][b]], [bX[xb][b]])
                    release_bank(bks[b][0])
            for b in range(NB):
                r0 = ti * T + b * 128
                P.dma(lambda h, r0=r0, b=b: h.dma_start(out=y_d[r0:r0 + 128, :], in_=X[xb][:, b, :]),
                      [bX[xb][b]], [by_blk[xb][b]], eng="pool", sem_buf=bX[xb][b])

        load_x(0)
        run(norm1_stream(0))
        for ti in range(ntile):
            if ti + 1 < ntile:
                load_x(ti + 1)
            if ti > 0:
                for kv in range(2):
                    CP("pool", knz[kv][:, 0, :], knz[kv][:, NB, :], [bkn[NB]], [bkn[0]])
                    CP("pool", vtz[kv][:, 0, :], vtz[kv][:, NB, :], [bvt[NB]], [bvt[0]])
            phase_tanh(ti)
            merge(proj2_stream(ti), algebra_stream(ti, (0, 2)), algebra_stream(ti, (1, 3)))
            merge(hgA_stream(ti), hgB_stream(ti), swa_block_stream(ti), gates_stream(ti), delay=[0, 2, 0, 0])
            branches(ti)
            wout_norm2(ti)
            ffn_up(ti, range(10))
            W10, bW10 = piece(ti, P_UP0 + 10)
            gd = ffn_down_stream(ti, hold_first=P_UP0 + 10)
            for _ in range(NB):
                next(gd)
            ffn_up_chunk(ti, W10, bW10, 20, 0)
            ffn_up_chunk(ti, W10, bW10, 21, 1)
            if ti + 1 < ntile:
                merge(gd, norm1_stream(ti + 1), ratio=[5, 1])
            else:
                run(gd)
        fin = mk("fin")
        P.op("sp", lambda h: h.nop(), [b_ for l_ in by_blk for b_ in l_] + [bdbg], [fin])
        P.finalize()

        esems = {e: es.enter_context(nc.semaphore("sem_" + e)) for e in ENGS}
        dsems = [es.enter_context(nc.semaphore(f"dsem{i}")) for i in range(len(P.dma_sem_counts))]
        block = es.enter_context(nc.Block())

        @block.sync
        def _(h):
            P.emit("sp", h, esems, dsems)

        @block.scalar
        def _(h):
            P.emit("act", h, esems, dsems)

        @block.vector
        def _(h):
            P.emit("dve", h, esems, dsems)

        @block.gpsimd
        def _(h):
            P.emit("pool", h, esems, dsems)

        @block.tensor
        def _(h):
            P.emit("pe", h, esems, dsems)
    build_program.dbg_list = dbg_list
    build_program.stats = dict(sbuf_bytes=sb_total[0], n_ops={e: len(P.ops[e]) for e in ENGS}, n_dsems=len(P.dma_sem_counts))
    return nc


_CONST_KEYS = ("ident", "mask_st", "ones_bf", "blk64", "onesz", "scanmask", "dm")


def _in_maps(inputs, seq, n_cores):
    consts = _host_consts()
    x = np.asarray(inputs["x"], dtype=np.float32)
    shared = {}
    for k in ("norm1_g", "w_in", "hgrn_out_g", "q_norm_g", "k_norm_g", "attn_sinks", "w_branch_a", "w_branch_b",
              "w_out", "norm2_g", "w_up", "conv_w", "conv_b", "w_down"):
        a = np.asarray(inputs[k], dtype=np.float32)
        shared[k] = np.ascontiguousarray(a[0])
    shared["hgrn_lb_logits"] = np.ascontiguousarray(np.asarray(inputs["hgrn_lb_logits"], dtype=np.float32))
    for k in _CONST_KEYS:
        shared["c_" + k] = consts[k]
    maps = []
    for c in range(n_cores):
        m = dict(shared)
        m["x"] = np.ascontiguousarray(x[c, :seq])
        maps.append(m)
    return maps


def kernel(**inputs):
    nc = build_program(SEQ)
    maps = _in_maps(inputs, SEQ, N_CORES)
    res = run_bass_kernel_spmd(nc, maps, core_ids=list(range(N_CORES)))
    out = np.stack([np.asarray(r["y"], dtype=np.float32) for r in res.results], axis=0)
    return out
```

```python
from contextlib import ExitStack

import numpy as np
import ml_dtypes

import concourse.bass as bass
import concourse.mybir as mybir
from concourse.bass_utils import run_bass_kernel_spmd

F32 = mybir.dt.float32
BF16 = mybir.dt.bfloat16
AF = mybir.ActivationFunctionType
ALU = mybir.AluOpType

N_CORES = 8
D = 1024
SEQ = 8192
T = 512
NB = T // 128
DFF = 2816
NFC = DFF // 128
EPS = 1e-6
NSLOT = 5
SAME_ENGINE_DIST = 4


def _needs_sync(p, o):
    if p.eng != o.eng:
        return True
    if o.eng in ("pe", "sp"):
        return False
    return (o.idx - p.idx) <= SAME_ENGINE_DIST

ENGS = ("pe", "act", "dve", "pool", "sp")


class Buf:
    __slots__ = ("name", "w", "r", "dsem", "dcount", "alias")

    def __init__(self, name):
        self.name = name
        self.w = None
        self.r = []
        self.dsem = None
        self.dcount = 0
        self.alias = []


class Op:
    __slots__ = ("eng", "fn", "deps", "signals", "sigval", "idx", "dma_ev")

    def __init__(self, eng, fn):
        self.eng = eng
        self.fn = fn
        self.deps = []
        self.signals = False
        self.sigval = 0
        self.idx = 0
        self.dma_ev = None


class Prog:
    def __init__(self):
        self.ops = {e: [] for e in ENGS}
        self.dma_sem_counts = []

    def _deps_for(self, reads, writes):
        deps = []
        for b in reads:
            if b.w is not None:
                deps.append(b.w)
        for b in writes:
            if b.w is not None:
                deps.append(b.w)
            deps.extend(b.r)
            for a in b.alias:
                if a.w is not None:
                    deps.append(a.w)
                deps.extend(a.r)
        return deps

    def _commit(self, ev, reads, writes):
        for b in reads:
            b.r.append(ev)
        for b in writes:
            b.w = ev
            b.r = []

    def op(self, eng, fn, reads=(), writes=()):
        o = Op(eng, fn)
        o.deps = self._deps_for(reads, writes)
        o.idx = len(self.ops[eng])
        self.ops[eng].append(o)
        self._commit(("op", o), reads, writes)
        return o

    def dma(self, fn, reads=(), writes=(), eng="sp", sem_buf=None):
        o = Op(eng, fn)
        b = sem_buf or (writes[0] if writes else reads[0])
        if b.dsem is None:
            b.dsem = len(self.dma_sem_counts)
            self.dma_sem_counts.append(0)
        deps = []
        for rb in reads:
            if rb.w is not None:
                deps.append(rb.w)
        for wb in writes:
            if wb.w is not None and not (wb.w[0] == "dma" and wb.w[1] == b.dsem):
                deps.append(wb.w)
            deps.extend(wb.r)
            for a in wb.alias:
                if a.w is not None:
                    deps.append(a.w)
                deps.extend(a.r)
        o.deps = deps
        o.idx = len(self.ops[eng])
        self.ops[eng].append(o)
        self.dma_sem_counts[b.dsem] += 16
        ev = ("dma", b.dsem, self.dma_sem_counts[b.dsem])
        o.dma_ev = ev
        self._commit(ev, reads, writes)
        return o

    def fence(self, engs=("pe", "act", "dve", "pool")):
        fb = [Buf("fence_" + e) for e in engs]
        for e, b in zip(engs, fb):
            self.op(e, lambda h: h.drain(), [], [b])
        for e in engs:
            self.op(e, lambda h: h.nop(), fb, [])

    def finalize(self):
        for e in ENGS:
            for o in self.ops[e]:
                for d in o.deps:
                    if d[0] == "op":
                        p = d[1]
                        if _needs_sync(p, o):
                            p.signals = True
        for e in ENGS:
            c = 0
            for o in self.ops[e]:
                if o.signals:
                    c += 1
                    o.sigval = c

    def emit(self, eng, h, esems, dsems):
        seen = {}
        for o in self.ops[eng]:
            need = {}
            for d in o.deps:
                if d[0] == "op":
                    p = d[1]
                    if not _needs_sync(p, o):
                        continue
                    key = ("e", p.eng)
                    val = p.sigval
                else:
                    key = ("d", d[1])
                    val = d[2]
                if val > need.get(key, 0):
                    need[key] = val
            for key, val in need.items():
                if seen.get(key, 0) >= val:
                    continue
                seen[key] = val
                sem = esems[key[1]] if key[0] == "e" else dsems[key[1]]
                h.wait_ge(sem, val)
            inst = o.fn(h)
            if o.dma_ev is not None:
                inst.then_inc(dsems[o.dma_ev[1]], 16)
            elif o.signals:
                inst.then_inc(esems[eng], 1)


def _host_consts():
    c = {}
    c["ident"] = np.eye(128, dtype=np.float32).astype(ml_dtypes.bfloat16)
    s = np.arange(128)[:, None]
    t = np.arange(128)[None, :]
    c["mask_st"] = (s <= t).astype(np.float32)
    c["ones_bf"] = np.ones((128, 128), dtype=np.float32).astype(ml_dtypes.bfloat16)
    blk = np.zeros((128, 128), dtype=np.float32)
    blk[:64, :64] = 1.0
    blk[64:, 64:] = 1.0
    c["blk64"] = blk.astype(ml_dtypes.bfloat16)
    oz = np.zeros((128, 2, 128), dtype=np.float32)
    oz[:, 0, :64] = 1.0
    oz[:, 1, 64:] = 1.0
    c["onesz"] = oz.astype(ml_dtypes.bfloat16)
    sm = np.ones((128, T), dtype=np.float32)
    sm[:, 0::128] = 0.0
    c["scanmask"] = sm
    dm = np.zeros((128, 2, 2, 4, 128), dtype=np.float64)
    for kv in range(2):
        for g in range(4):
            hh = kv * 4 + g
            slope = 2.0 ** (-8.0 * (hh + 1) / 8.0)
            d_cur = (t - s).astype(np.float64)
            d_prev = d_cur + 128.0
            dm[:, kv, 1, g, :] = np.where(d_cur >= 0, -slope * d_cur, -30000.0)
            dm[:, kv, 0, g, :] = np.where(d_prev < 128, -slope * d_prev, -30000.0)
    c["dm"] = dm.astype(np.float32)
    return c


(P_Q, P_G, P_F, P_I, P_AQ, P_AKV, P_GAB0, P_GAB1, P_GAB2, P_GAB3, P_WAB0, P_WAB1, P_WO0) = range(13)
P_GAB = (P_GAB0, P_GAB1, P_GAB2, P_GAB3)
P_WAB = (P_WAB0, P_WAB1)
P_UP0 = P_WO0 + 2
P_DN0 = P_UP0 + 11
NPIECE = P_DN0 + 6


def build_program(seq=SEQ, debug=False):
    ntile = seq // T
    nc = bass.Bass("TRN2", target_bir_lowering=False)
    dt_in = lambda name, shape, dt=F32: nc.dram_tensor(name, list(shape), dt, kind="ExternalInput").ap()
    x_d = dt_in("x", [seq, D])
    norm1_g = dt_in("norm1_g", [D])
    w_in = dt_in("w_in", [D, 4864])
    lb_logits = dt_in("hgrn_lb_logits", [2, 512])
    out_g = dt_in("hgrn_out_g", [128])
    qn_g = dt_in("q_norm_g", [64])
    kn_g = dt_in("k_norm_g", [64])
    sinks = dt_in("attn_sinks", [8])
    w_a = dt_in("w_branch_a", [512, D])
    w_b = dt_in("w_branch_b", [512, D])
    w_out = dt_in("w_out", [D, D])
    norm2_g = dt_in("norm2_g", [D])
    w_up = dt_in("w_up", [D, 2 * DFF])
    conv_w = dt_in("conv_w", [3, DFF])
    conv_b = dt_in("conv_b", [DFF])
    w_down = dt_in("w_down", [DFF, D])
    c_ident = dt_in("c_ident", [128, 128], BF16)
    c_mask = dt_in("c_mask_st", [128, 128])
    c_ones = dt_in("c_ones_bf", [128, 128], BF16)
    c_blk = dt_in("c_blk64", [128, 128], BF16)
    c_onesz = dt_in("c_onesz", [128, 2, 128], BF16)
    c_scan = dt_in("c_scanmask", [128, T])
    c_dm = dt_in("c_dm", [128, 2, 2, 4, 128])
    y_d = nc.dram_tensor("y", [seq, D], F32, kind="ExternalOutput").ap()
    wscr = nc.dram_tensor("wscr", [NPIECE, 128, 4096], BF16, kind="Internal").ap()
    dbg = {}
    if debug:
        dbg["x1"] = nc.dram_tensor("dbg_x1", [seq, D], F32, kind="ExternalOutput").ap()

    P = Prog()
    with ExitStack() as es:
        sb_total = [0]

        def sb(name, shape, dt):
            n = 1
            for s_ in shape[1:]:
                n *= s_
            sb_total[0] += n * (4 if dt == F32 else 2)
            return es.enter_context(nc.sbuf_tensor(name, list(shape), dt))

        X = [sb(f"X{i}", [128, NB, D], F32) for i in range(2)]
        hT = sb("hT", [128, 8, T], BF16)
        xs = [sb(f"xs{i}", [128, D], BF16) for i in range(2)]
        wslot = [sb(f"wslot{i}", [128, 4096], BF16) for i in range(NSLOT)]
        ident = sb("ident", [128, 128], BF16)
        mask_st = sb("mask_st", [128, 128], F32)
        ones_bf = sb("ones_bf", [128, 128], BF16)
        blk64 = sb("blk64", [128, 128], BF16)
        onesz = sb("onesz", [128, 2, 128], BF16)
        scanmask = sb("scanmask", [128, T], F32)
        dm = sb("dm", [128, 2, 2, 512], F32)
        g1 = sb("g1", [128, 8], F32)
        g2 = sb("g2", [128, 8], F32)
        lbl = sb("lbl", [128, 2, 4], F32)
        hco = sb("hco", [128, 4], F32)
        nhco = sb("nhco", [128, 4], F32)
        og = sb("og", [128, 1], F32)
        qg = sb("qg", [128, 1], F32)
        kg = sb("kg", [128, 1], F32)
        gqrow = sb("gqrow", [128, 64], F32)
        gkrow = sb("gkrow", [128, 64], F32)
        mq = sb("mq", [128, 4], F32)
        negshift = sb("negshift", [128, 1], F32)
        sinkraw = sb("sinkraw", [128, 4], F32)
        sinkexp = sb("sinkexp", [128, 4], F32)
        cw = sb("cw", [128, 3, NFC], F32)
        cb = sb("cb", [128, NFC], F32)
        epsc = sb("epsc", [128, 1], F32)
        mhalf = sb("mhalf", [128, 4], F32)
        ss = sb("ss", [128, NB], F32)
        ms = sb("ms", [128, NB], F32)
        rstd = sb("rstd", [128, NB], F32)
        nref = sb("nref", [128, 4, NB], F32)
        s1 = sb("s1", [128, 4, NB], F32)
        ddif = sb("ddif", [128, 4, NB], F32)
        s2 = sb("s2", [128, 4, NB], F32)
        dec = sb("dec", [128, 4, NB], F32)
        f32t = [sb(f"f32t{i}", [128, T], F32) for i in range(8)]
        tht = [f32t[0], f32t[1]]
        logf = [f32t[2], f32t[3]]
        cum = [f32t[4], f32t[5]]
        e1 = [f32t[6], f32t[7]]
        e3 = [f32t[2], f32t[3]]
        Ef = [f32t[2], f32t[3]]
        t1 = [f32t[4], f32t[5]]
        lnt = [f32t[0], f32t[1]]
        rden = f32t[7]
        bf32t = [Buf(f"f32t{i}") for i in range(8)]
        btht = [bf32t[0], bf32t[1]]
        blogf = [bf32t[2], bf32t[3]]
        bcum = [bf32t[4], bf32t[5]]
        be1 = [bf32t[6], bf32t[7]]
        be3 = [bf32t[2], bf32t[3]]
        bEf = [bf32t[2], bf32t[3]]
        bt1 = [bf32t[4], bf32t[5]]
        blnt = [bf32t[0], bf32t[1]]
        brden = bf32t[7]
        junk = f32t[1][:, :].bitcast(BF16)
        bjunk = bf32t[1]
        ARENA = 19472
        arena = sb("arena", [128, ARENA], BF16)

        def aview(off, n, dt, pat=None, **kw):
            v = arena[:, off:off + n]
            if dt == F32:
                v = v.bitcast(F32)
            if pat:
                v = v.rearrange(pat, **kw)
            return v

        O_VTM, O_QI, O_QD, O_KD, O_KI, O_ST, O_QS, O_GS, O_KF = 0, 2048, 4096, 6144, 8192, 10240, 11264, 13312, 15360
        qi = aview(O_QI, 2048, BF16, "p (h t) -> p h t", t=T)
        qd = aview(O_QD, 2048, BF16, "p (h t) -> p h t", t=T)
        kd = aview(O_KD, 2048, BF16, "p (h t) -> p h t", t=T)
        ki = aview(O_KI, 2048, BF16, "p (h t) -> p h t", t=T)
        qs = aview(O_QS, 2048, BF16, "p (h t) -> p h t", t=T)
        kf = aview(O_KF, 4096, F32, "p (h t) -> p h t", t=T)
        gs = aview(O_GS, 2048, BF16, "p (h t) -> p h t", t=T)
        vtm = aview(O_VTM, 2048, BF16, "p (b v) -> p b v", v=512)
        sT = [aview(O_ST + i * 512, 512, BF16, "p (h t) -> p h t", t=128) for i in range(2)]
        assert O_KF + 4096 <= ARENA
        act = aview(0, NFC * T, BF16, "p (c t) -> p c t", t=T)
        G = [aview(NFC * T + i * 2056, 2056, F32) for i in range(2)]
        c1 = [aview(NFC * T + 4112 + i * 1024, 1024, F32) for i in range(2)]
        c2 = [aview(NFC * T + 4112 + 2048 + i * 1024, 1024, F32) for i in range(2)]
        assert NFC * T + 4112 + 4096 <= ARENA

        kitm = [sb(f"kitm{i}", [128, 4, 128], BF16) for i in range(2)]
        S32 = sb("S32", [128, 4, 128], F32)
        Sb = sb("Sb", [128, 4, 128], BF16)
        sqo = [sb(f"sqo{i}", [128, 512], BF16) for i in range(2)]
        AT = sb("AT", [128, 4, T], BF16)
        qn = sb("qn", [128, 4, T], BF16)
        knz = [sb(f"knz{i}", [128, NB + 1, 128], BF16) for i in range(2)]
        vtz = [sb(f"vtz{i}", [128, NB + 1, 128], BF16) for i in range(2)]
        Ep = [sb(f"Ep{i}", [128, 512], BF16) for i in range(4)]
        BT = sb("BT", [128, 4, T], BF16)
        m1 = [sb(f"m1_{i}", [128, T], BF16) for i in range(2)]
        m2 = [sb(f"m2_{i}", [128, T], BF16) for i in range(2)]
        mixT = sb("mixT", [128, 8, T], BF16)
        gbuf = sb("gbuf", [128, 8, T], BF16)
        halo = sb("halo", [128, NFC, 2], F32)
        stage = [X[1][:, 0:2, :].rearrange("p b d -> p (b d)"), X[1][:, 2:4, :].rearrange("p b d -> p (b d)")]

        banks = [es.enter_context(nc.psum_tensor(f"bank{i}", [128, 512], F32)) for i in range(8)]

        def mk(name):
            return Buf(name)

        bX = [[mk(f"X{i}b{b}") for b in range(NB)] for i in range(2)]
        bhT = [mk(f"hT{b}") for b in range(NB)]
        bxs = [mk("xs0"), mk("xs1")]
        bslot = [mk(f"slot{i}") for i in range(NSLOT)]
        bconst = mk("const")
        bwscr = mk("wscr")
        bss, bms, brstd = mk("ss"), mk("ms"), mk("rstd")
        bqs = [mk(f"qs{h}") for h in range(4)]
        bkf = [mk(f"kf{h}") for h in range(4)]
        bgs = [mk(f"gs{h}") for h in range(4)]
        bvtm = [mk(f"vtm{b}") for b in range(NB)]
        bsm = [mk(f"small{h}") for h in range(4)]
        bqi = [mk(f"qi{h}") for h in range(4)]
        bqd = [mk(f"qd{h}") for h in range(4)]
        bkd = [mk(f"kd{h}") for h in range(4)]
        bki = [mk(f"ki{h}") for h in range(4)]
        bsT = [mk("sT0"), mk("sT1")]
        bkitm = [mk("kitm0"), mk("kitm1")]
        bS32 = [mk(f"S32_{h}") for h in range(4)]
        bSb = [mk(f"Sb{h}") for h in range(4)]
        bsqo = [mk("sqo0"), mk("sqo1")]
        bAT = [mk(f"AT{b}") for b in range(NB)]
        bqn = [mk(f"qn{j}") for j in range(4)]
        bkn = [mk(f"kn{b}") for b in range(NB + 1)]
        bvt = [mk(f"vt{b}") for b in range(NB + 1)]
        bEp = [mk(f"Ep{i}") for i in range(4)]
        bBT = [mk(f"BT{b}") for b in range(NB)]
        bm1 = [mk("m1_0"), mk("m1_1")]
        bm2 = [mk("m2_0"), mk("m2_1")]
        bmix = [mk(f"mix{c}") for c in range(8)]
        bgb = [mk(f"gb{c}") for c in range(8)]
        bG = [mk("G0"), mk("G1")]
        bc1 = [mk("c1_0"), mk("c1_1")]
        bc2 = [mk("c2_0"), mk("c2_1")]
        bhalo = [mk(f"halo{c}") for c in range(NFC)]
        bact = [mk(f"act{c}") for c in range(NFC)]
        bstage = [[bX[1][0], bX[1][1]], [bX[1][2], bX[1][3]]]
        bbank = [mk(f"bank{i}") for i in range(8)]
        by = mk("y")
        bdbg = mk("dbg")

        arena_map = []
        for h in range(4):
            arena_map += [(bqi[h], O_QI + h * 512, 512), (bqd[h], O_QD + h * 512, 512), (bkd[h], O_KD + h * 512, 512),
                          (bki[h], O_KI + h * 512, 512), (bqs[h], O_QS + h * 512, 512), (bkf[h], O_KF + h * 1024, 1024),
                          (bgs[h], O_GS + h * 512, 512)]
        for b in range(NB):
            arena_map.append((bvtm[b], O_VTM + b * 512, 512))
        for i in range(2):
            arena_map += [(bsT[i], O_ST + i * 512, 512), (bG[i], NFC * T + i * 2056, 2056),
                          (bc1[i], NFC * T + 4112 + i * 1024, 1024), (bc2[i], NFC * T + 4112 + 2048 + i * 1024, 1024)]
        for c in range(NFC):
            arena_map.append((bact[c], c * T, T))
        bstage23 = [mk("stage2"), mk("stage3")]
        arena_map += [(bstage23[0], 0, 4096), (bstage23[1], 4096, 4096)]
        for i_, (b1, o1, n1) in enumerate(arena_map):
            for (b2, o2, n2) in arena_map[i_ + 1:]:
                if o1 < o2 + n2 and o2 < o1 + n1:
                    b1.alias.append(b2)
                    b2.alias.append(b1)

        bank_ctr = [0]

        bank_reserved = set()

        def nextbank(reserve=False):
            while True:
                i = bank_ctr[0] % 8
                bank_ctr[0] += 1
                if i not in bank_reserved:
                    break
            if reserve:
                bank_reserved.add(i)
            return banks[i], bbank[i]

        def release_bank(bk):
            bank_reserved.discard(banks.index(bk))

        def A(out, in_, func, reads, writes, scale=1.0, bias=0.0, accum=None):
            if accum is None:
                P.op("act", lambda h: h.activation(out=out, in_=in_, func=func, scale=scale, bias=bias), reads, writes)
            else:
                P.op("act", lambda h: h.activation(out=out, in_=in_, func=func, scale=scale, bias=bias, accum_out=accum), reads, writes)

        def TS(eng, out, in0, s1_, s2_, op0, op1, reads, writes):
            if s2_ is None:
                P.op(eng, lambda h: h.tensor_scalar(out=out, in0=in0, scalar1=s1_, scalar2=None, op0=op0), reads, writes)
            else:
                P.op(eng, lambda h: h.tensor_scalar(out=out, in0=in0, scalar1=s1_, scalar2=s2_, op0=op0, op1=op1), reads, writes)

        def STT(out, in0, scalar, in1, op0, op1, reads, writes):
            P.op("dve", lambda h: h.scalar_tensor_tensor(out=out, in0=in0, scalar=scalar, in1=in1, op0=op0, op1=op1), reads, writes)

        def TT(eng, out, in0, in1, op, reads, writes):
            P.op(eng, lambda h: h.tensor_tensor(out=out, in0=in0, in1=in1, op=op), reads, writes)

        def CP(eng, out, in_, reads, writes):
            if eng == "act":
                P.op("act", lambda h: h.activation(out=out, in_=in_, func=AF.Copy), reads, writes)
            else:
                P.op(eng, lambda h: h.tensor_copy(out=out, in_=in_), reads, writes)

        def MM(out, lhsT, rhs, start, stop, reads, writes):
            P.op("pe", lambda h: h.matmul(out, lhsT=lhsT, rhs=rhs, start=start, stop=stop), reads, writes)

        def TR(out, in_, reads, writes):
            P.op("pe", lambda h: h.transpose(out=out, in_=in_, identity=ident[:]), list(reads) + [bconst], writes)

        def DMA(out, in_, reads, writes, sem_buf=None, slow=False):
            if slow:
                P.dma(lambda h: h.dma_start(out=out, in_=in_, allow_slow_non_contiguous=True), reads, writes, sem_buf=sem_buf)
            else:
                P.dma(lambda h: h.dma_start(out=out, in_=in_), reads, writes, sem_buf=sem_buf)

        dbg_list = []

        def DBG(name, ap, reads, dt=F32):
            if not debug:
                return
            shape = list(ap.shape)
            d_ = nc.dram_tensor("dbg_" + name, shape, dt, kind="ExternalOutput").ap()
            DMA(d_, ap, reads, [bdbg])
            dbg_list.append(name)

        DMA(ident[:], c_ident, [], [bconst])
        DMA(mask_st[:], c_mask, [], [bconst])
        DMA(ones_bf[:], c_ones, [], [bconst])
        DMA(blk64[:], c_blk, [], [bconst])
        DMA(onesz[:], c_onesz, [], [bconst])
        DMA(scanmask[:], c_scan, [], [bconst])
        DMA(dm[:], c_dm.rearrange("p a b g t -> p a b (g t)"), [], [bconst])
        DMA(g1[:], norm1_g.rearrange("(c p) -> p c", p=128), [], [bconst], slow=True)
        DMA(g2[:], norm2_g.rearrange("(c p) -> p c", p=128), [], [bconst], slow=True)
        DMA(lbl[:], lb_logits.rearrange("j (h p) -> p j h", p=128), [], [bconst], slow=True)
        DMA(og[:], out_g.rearrange("(p o) -> p o", o=1), [], [bconst], slow=True)
        for half in range(2):
            DMA(qg[half * 64:(half + 1) * 64, :], qn_g.rearrange("(p o) -> p o", o=1), [], [bconst], slow=True)
            DMA(kg[half * 64:(half + 1) * 64, :], kn_g.rearrange("(p o) -> p o", o=1), [], [bconst], slow=True)
            DMA(sinkraw[half * 64:(half + 1) * 64, :],
                sinks[half * 4:(half + 1) * 4].rearrange("(o g) -> o g", o=1).broadcast_to([64, 4]), [], [bconst], slow=True)
        DMA(gqrow[:], qn_g.rearrange("(o g) -> o g", o=1).broadcast_to([128, 64]), [], [bconst], slow=True)
        DMA(gkrow[:], kn_g.rearrange("(o g) -> o g", o=1).broadcast_to([128, 64]), [], [bconst], slow=True)
        DMA(cw[:], conv_w.rearrange("k (c p) -> p k c", p=128), [], [bconst], slow=True)
        DMA(cb[:], conv_b.rearrange("(c p) -> p c", p=128), [], [bconst], slow=True)

        bprm = mk("params")
        TT("dve", hco[:], lbl[:, 0, :], lbl[:, 1, :], ALU.subtract, [bconst], [bprm])
        A(hco[:], hco[:], AF.Tanh, [bprm], [bprm], scale=0.5)
        TS("dve", nhco[:], hco[:], 0.25, -0.25, ALU.mult, ALU.add, [bprm], [bprm])
        TS("dve", hco[:], nhco[:], -1.0, None, ALU.mult, None, [bprm], [bprm])
        TS("dve", og[:], og[:], 0.5, None, ALU.mult, None, [bconst], [bprm])
        TS("dve", qg[:], qg[:], 0.125, None, ALU.mult, None, [bconst], [bprm])
        for i_, grow in enumerate((gqrow, gkrow)):
            P.op("dve", lambda h, i_=i_, grow=grow: h.tensor_reduce(out=mq[:, i_:i_ + 1], in_=grow[:], op=ALU.max, axis=mybir.AxisListType.X), [bconst], [bprm])
            P.op("dve", lambda h, i_=i_, grow=grow: h.tensor_reduce(out=mq[:, 2 + i_:3 + i_], in_=grow[:], op=ALU.min, axis=mybir.AxisListType.X), [bconst], [bprm])
            TS("dve", mq[:, 2 + i_:3 + i_], mq[:, 2 + i_:3 + i_], -1.0, None, ALU.mult, None, [bprm], [bprm])
            TT("dve", mq[:, i_:i_ + 1], mq[:, i_:i_ + 1], mq[:, 2 + i_:3 + i_], ALU.max, [bprm], [bprm])
        STT(negshift[:], mq[:, 0:1], -8.0, mq[:, 1:2], ALU.mult, ALU.mult, [bprm], [bprm])
        A(sinkexp[:], sinkraw[:], AF.Exp, [bconst, bprm], [bprm], bias=negshift[:, 0:1])
        P.op("pool", lambda h: h.memset(epsc[:], EPS), [], [bprm])
        P.op("pool", lambda h: h.memset(mhalf[:], -0.5), [], [bprm])
        P.op("pool", lambda h: h.memset(S32[:], 0.0), [], bS32)
        P.op("pool", lambda h: h.memset(Sb[:], 0.0), [], bSb)
        P.op("pool", lambda h: h.memset(halo[:], 0.0), [], bhalo)
        for i_ in range(2):
            P.op("pool", lambda h, i_=i_: h.memset(knz[i_][:], 0.0), [], bkn)
            P.op("pool", lambda h, i_=i_: h.memset(vtz[i_][:], 0.0), [], bvt)

        cast_ctr = [0]
        bwp = [[mk(f"wscr{i}_{h}") for h in range(2)] for i in range(NPIECE)]
        stage4 = stage + [arena[:, 0:4096].bitcast(F32), arena[:, 4096:8192].bitcast(F32)]
        bstage4 = bstage + [[bstage23[0]], [bstage23[1]]]

        cast_jobs = []
        bslq = [[mk(f"slot{i}q{q}") for q in range(4)] for i in range(NSLOT)]

        def cast_half(pi, half, load_list, scale=None, perm=None):
            cast_jobs.append((pi, half, load_list, scale, perm))

        def cast_loads(k):
            pi, half, load_list, scale, perm = cast_jobs[k]
            st, bst = stage4[k % 4], bstage4[k % 4]
            st3 = st.rearrange("p (q n) -> p q n", q=4)
            for dstf, src in load_list:
                DMA(dstf(st3), src, [], bst, sem_buf=bst[0], slow=True)

        def cast_compute(k):
            pi, half, load_list, scale, perm = cast_jobs[k]
            st, bst = stage4[k % 4], bstage4[k % 4]
            sidx = k % NSLOT
            dst = wslot[sidx][:, 0:2048]
            for q in range(4):
                kc = half * 4 + q
                o_ap = dst[:, q * 512:(q + 1) * 512]
                i_ap = st[:, q * 512:(q + 1) * 512]
                if perm is not None:
                    o_ap, i_ap = perm(o_ap, i_ap)
                wr = [bslq[sidx][q]]
                if scale is not None and q % 2 == 0:
                    A(o_ap, i_ap, AF.Copy, bst + [bconst], wr, scale=scale[:, kc:kc + 1])
                elif scale is not None:
                    TS("dve", o_ap, i_ap, scale[:, kc:kc + 1], None, ALU.mult, None, bst + [bconst], wr)
                elif (k + q) % 2 == 0:
                    CP("dve", o_ap, i_ap, bst, wr)
                else:
                    CP("act", o_ap, i_ap, bst, wr)

        def cast_store(k):
            pi, half = cast_jobs[k][0], cast_jobs[k][1]
            sidx = k % NSLOT
            DMA(wscr[pi, :, half * 2048:(half + 1) * 2048], wslot[sidx][:, 0:2048], [bslot[sidx]] + bslq[sidx], [bwp[pi][half]], sem_buf=bslot[sidx])

        def run_cast_jobs(ahead=3):
            n = len(cast_jobs)
            for k in range(min(ahead, n)):
                cast_loads(k)
            for k in range(n):
                cast_compute(k)
                if k + ahead < n:
                    cast_loads(k + ahead)
                cast_store(k)

        def rows4(w, r0, c0, ncols, nq=4):
            return w[r0:r0 + nq * 128, c0:c0 + ncols].rearrange("(q p) n -> p q n", p=128)

        def std_piece(pi, w, c0, ncols, scale=None):
            for half in range(2):
                cast_half(pi, half, [(lambda st3: st3[:, :, 0:ncols], rows4(w, half * 512, c0, ncols))], scale)

        HG_Q0, HG_F0, HG_I0, HG_G0, AT_Q0, AT_K0, AT_V0, GA0, GB0 = 0, 512, 1024, 1536, 2048, 2560, 2688, 2816, 3840
        std_piece(P_Q, w_in, HG_Q0, 512, g1)
        std_piece(P_G, w_in, HG_G0, 512, g1)
        std_piece(P_F, w_in, HG_F0, 512, g1)
        std_piece(P_I, w_in, HG_I0, 512, g1)
        perm_q = lambda o_ap, i_ap: (o_ap.rearrange("p (j kv d) -> p j kv d", j=4, kv=2, d=64),
                                     i_ap.rearrange("p (kv j d) -> p j kv d", kv=2, j=4, d=64))
        for half in range(2):
            cast_half(P_AQ, half, [(lambda st3: st3[:, :, :], rows4(w_in, half * 512, AT_Q0, 512))], g1, perm=perm_q)
        std_piece(P_AKV, w_in, AT_K0, 256, g1)
        for hf in range(2):
            cast_half(P_WAB[hf], 0, [(lambda st3: st3[:, :, :], rows4(w_a, 0, hf * 512, 512))])
            ll = []
            for g in range(4):
                for kv in range(2):
                    r0 = (kv * 4 + g) * 64
                    ll.append((lambda st3, g=g, kv=kv: st3[kv * 64:(kv + 1) * 64, g, :], w_b[r0:r0 + 64, hf * 512:(hf + 1) * 512]))
            cast_half(P_WAB[hf], 1, ll)
        perm_pair = lambda o_ap, i_ap: (o_ap.rearrange("p (cc xy e) -> p cc xy e", cc=2, xy=2, e=128),
                                        i_ap.rearrange("p (xy cc e) -> p cc xy e", xy=2, cc=2, e=128))
        for k4 in range(4):
            for half in range(2):
                cast_half(P_GAB[k4], half,
                          [(lambda st3: st3[:, :, 0:256], rows4(w_in, half * 512, GA0 + k4 * 256, 256)),
                           (lambda st3: st3[:, :, 256:512], rows4(w_in, half * 512, GB0 + k4 * 256, 256))], g1, perm=perm_pair)
        std_piece(P_WO0, w_out, 0, 512)
        std_piece(P_WO0 + 1, w_out, 512, 512)
        for p in range(11):
            for half in range(2):
                cast_half(P_UP0 + p, half,
                          [(lambda st3: st3[:, :, 0:256], rows4(w_up, half * 512, p * 256, 256)),
                           (lambda st3: st3[:, :, 256:512], rows4(w_up, half * 512, DFF + p * 256, 256))], g2, perm=perm_pair)
        for hf in range(2):
            for kgp in range(3):
                pi = P_DN0 + hf * 3 + kgp
                nk = 8 if kgp < 2 else NFC - 16
                for half in range(2):
                    nq = min(4, nk - half * 4)
                    if nq > 0:
                        cast_half(pi, half, [(lambda st3, nq=nq: st3[:, 0:nq, :], rows4(w_down, (kgp * 8 + half * 4) * 128, hf * 512, 512, nq))])

        run_cast_jobs()

        stream_issued = [0]
        total_stream = ntile * NPIECE

        def issue_loads_upto(gidx):
            while stream_issued[0] <= min(gidx, total_stream - 1):
                k = stream_issued[0]
                DMA(wslot[k % NSLOT][:, :], wscr[k % NPIECE], bwp[k % NPIECE], [bslot[k % NSLOT]])
                stream_issued[0] += 1

        last_piece = [-1]

        def piece(ti, pi, hold=None):
            gidx = ti * NPIECE + pi
            assert gidx >= last_piece[0]
            last_piece[0] = gidx
            oldest = gidx if hold is None else ti * NPIECE + hold
            assert gidx <= oldest + NSLOT - 1
            issue_loads_upto(oldest + NSLOT - 1)
            return wslot[gidx % NSLOT], bslot[gidx % NSLOT]

        bss_b = [mk(f"ss{b}") for b in range(NB)]
        brstd_b = [mk(f"rstd{b}") for b in range(NB)]
        by_blk = [[mk(f"y{i}_{b}") for b in range(NB)] for i in range(2)]

        def load_x(ti):
            xb = ti % 2
            for b in range(NB):
                r0 = ti * T + b * 128
                DMA(X[xb][:, b, :], x_d[r0:r0 + 128, :], [], [bX[xb][b]])

        def norm_stats(xb, b):
            A(junk, X[xb][:, b, :], AF.Square, [bX[xb][b]], [bjunk, bss_b[b]], accum=ss[:, b:b + 1])
            TS("dve", ms[:, b:b + 1], ss[:, b:b + 1], 1.0 / D, EPS, ALU.mult, ALU.add, [bss_b[b]], [bss_b[b]])
            TT("pool", rstd[:, b:b + 1], ms[:, b:b + 1], mhalf[:, 0:1], ALU.pow, [bss_b[b], bprm], [brstd_b[b]])
            k = b % 2
            TS("dve", xs[k][:], X[xb][:, b, :], rstd[:, b:b + 1], None, ALU.mult, None, [bX[xb][b], brstd_b[b]], [bxs[k]])

        def norm_transpose(xb, b):
            k = b % 2
            for half in range(2):
                bk, bbk = nextbank()
                for c4 in range(4):
                    c = half * 4 + c4
                    MM(bk[:, c4 * 128:(c4 + 1) * 128], xs[k][:, c * 128:(c + 1) * 128], ident[:], True, True, [bxs[k], bconst], [bbk])
                CP("act", hT[:, half * 4:(half + 1) * 4, b * 128:(b + 1) * 128], bk[:].rearrange("p (c t) -> p c t", t=128), [bbk], [bhT[b]])

        def norm_block(xb, b):
            norm_stats(xb, b)
            norm_transpose(xb, b)

        def norm1_stream(ti):
            for b in range(NB):
                norm_block(ti % 2, b)
                yield

        def fm_group(W, bW, c0, ncol=128, kcs=8, wstride=512):
            bk, bbk = nextbank()
            for kc in range(kcs):
                MM(bk[0:ncol, :], W[:, kc * wstride + c0: kc * wstride + c0 + ncol], hT[:, kc, :], kc == 0, kc == kcs - 1,
                   [bW] + bhT, [bbk])
            return bk, bbk

        def merge(*gens, ratio=None, delay=None):
            gens = list(gens)
            ratio = ratio or [1] * len(gens)
            delay = list(delay or [0] * len(gens))
            alive = [True] * len(gens)
            while any(alive):
                for i, g in enumerate(gens):
                    if delay[i] > 0:
                        delay[i] -= 1
                        continue
                    for _ in range(ratio[i]):
                        if alive[i]:
                            try:
                                next(g)
                            except StopIteration:
                                alive[i] = False

        def run(g):
            for _ in g:
                pass

        def phase_tanh(ti):
            W, bW = piece(ti, P_Q)
            for h in range(4):
                bk, bbk = fm_group(W, bW, h * 128)
                k = h % 2
                A(tht[k][:], bk[:], AF.Tanh, [bbk], [btht[k]], scale=0.5)
                STT(qs[:, h, :], tht[k][:], 1.0, bk[:], ALU.add, ALU.mult, [btht[k], bbk], [bqs[h]])
            W, bW = piece(ti, P_G)
            for h in range(4):
                bk, bbk = fm_group(W, bW, h * 128)
                k = h % 2
                A(tht[k][:], bk[:], AF.Tanh, [bbk], [btht[k]], scale=0.5)
                STT(gs[:, h, :], tht[k][:], 1.0, bk[:], ALU.add, ALU.mult, [btht[k], bbk], [bgs[h]])
            W, bW = piece(ti, P_F)
            for h in range(4):
                bk, bbk = fm_group(W, bW, h * 128)
                k = h % 2
                A(tht[k][:], bk[:], AF.Tanh, [bbk], [btht[k]], scale=0.5)
                TS("dve", kf[:, h, :], tht[k][:], nhco[:, h:h + 1], hco[:, h:h + 1], ALU.mult, ALU.add, [btht[k], bprm], [bkf[h]])
            if ti == 0:
                DBG("qs", qs, bqs, BF16)
                DBG("kf", kf, bkf)
                DBG("gs", gs, bgs, BF16)

        def proj2_stream(ti):
            W, bW = piece(ti, P_I)
            for b in range(NB):
                bk, bbk = nextbank()
                for kc in range(8):
                    MM(bk[:], hT[:, kc, b * 128:(b + 1) * 128], W[:, kc * 512:(kc + 1) * 512], kc == 0, kc == 7, [bW, bhT[b]], [bbk])
                CP("act", vtm[:, b, :], bk[:], [bbk], [bvtm[b]])
                yield
            Wq, bWq = piece(ti, P_AQ)
            for j in range(5):
                k = j % 2
                if j < 4:
                    bk, bbk = fm_group(Wq, bWq, j * 128)
                else:
                    Wk, bWk = piece(ti, P_AKV)
                    bk, bbk = fm_group(Wk, bWk, 0)
                A(sqo[k][:], bk[:], AF.Square, [bbk], [bsqo[k]])
                bk2, bbk2 = nextbank()
                MM(bk2[:], blk64[:], sqo[k][:], True, True, [bconst, bsqo[k]], [bbk2])
                yield
                A(lnt[k][:], bk2[:], AF.Ln, [bbk2, bprm], [blnt[k]], scale=1.0 / 64, bias=epsc[:, 0:1])
                A(lnt[k][:], lnt[k][:], AF.Exp, [blnt[k]], [blnt[k]], scale=-0.5)
                if j < 4:
                    STT(qn[:, j, :], bk[:], qg[:, 0:1], lnt[k][:], ALU.mult, ALU.mult, [bbk, blnt[k], bprm], [bqn[j]])
                else:
                    for kv in range(2):
                        ps_ = slice(kv * 64, (kv + 1) * 64)
                        STT(knz[kv][ps_, 1:NB + 1, :], bk[ps_, :].rearrange("p (b t) -> p b t", t=128), kg[ps_, 0:1],
                            lnt[k][ps_, :].rearrange("p (b t) -> p b t", t=128), ALU.mult, ALU.mult, [bbk, blnt[k], bconst], bkn[1:])
                yield
            for b in range(NB):
                bk, bbk = nextbank()
                for kc in range(8):
                    MM(bk[:, 0:128], hT[:, kc, b * 128:(b + 1) * 128], Wk[:, kc * 512 + 128: kc * 512 + 256], kc == 0, kc == 7, [bWk, bhT[b]], [bbk])
                for kv in range(2):
                    CP("act", vtz[kv][:, b + 1, kv * 64:(kv + 1) * 64], bk[:, kv * 64:(kv + 1) * 64], [bbk], [bvt[b + 1]])
                if b % 2 == 1:
                    yield
            if ti == 0:
                DBG("vtm", vtm, bvtm, BF16)
                DBG("qn", qn[:], bqn, BF16)
                for kv in range(2):
                    DBG(f"knz{kv}", knz[kv][:], bkn, BF16)
                    DBG(f"vtz{kv}", vtz[kv][:], bvt, BF16)

        def algebra_stream(ti, heads=(0, 1, 2, 3)):
            for h in heads:
                k = h % 2
                A(logf[k][:], kf[:, h, :], AF.Ln, [bkf[h]], [blogf[k]], scale=-1.0, bias=1.0)
                P.op("dve", lambda hh, k=k: hh.tensor_tensor_scan(out=cum[k][:], data0=scanmask[:], data1=logf[k][:], initial=0.0,
                                                                   op0=ALU.mult, op1=ALU.add), [bconst, blogf[k]], [bcum[k]])
                yield
                cum3 = cum[k][:].rearrange("p (b t) -> p b t", t=128)
                TS("dve", nref[:, h, :], cum3[:, :, 63], -1.0, None, ALU.mult, None, [bcum[k]], [bsm[h]])
                TT("dve", ddif[:, h, :], cum3[:, :, 127], cum3[:, :, 63], ALU.subtract, [bcum[k]], [bsm[h]])
                A(e1[k][:], cum[k][:], AF.Exp, [bcum[k]], [be1[k]])
                A(s1[:, h, :], nref[:, h, :], AF.Exp, [bsm[h]], [bsm[h]])
                A(s2[:, h, :], ddif[:, h, :], AF.Exp, [bsm[h]], [bsm[h]])
                A(dec[:, h, :], cum3[:, :, 127], AF.Exp, [bcum[k]], [bsm[h]])
                STT(qi[:, h, :], qs[:, h, :], 0.5 * (128.0 ** -0.5), e1[k][:], ALU.mult, ALU.mult, [bqs[h], be1[k]], [bqi[h]])
                yield
                for b in range(NB):
                    sl = slice(b * 128, (b + 1) * 128)
                    A(e3[k][:, sl], cum[k][:, sl], AF.Exp, [bcum[k]], [be3[k]], scale=-1.0, bias=cum[k][:, b * 128 + 63: b * 128 + 64])
                for b in range(NB):
                    sl = slice(b * 128, (b + 1) * 128)
                    TS("dve", qd[:, h, sl], qi[:, h, sl], s1[:, h, b:b + 1], None, ALU.mult, None, [bqi[h], bsm[h]], [bqd[h]])
                yield
                TT("dve", kd[:, h, :], kf[:, h, :], e3[k][:], ALU.mult, [bkf[h], be3[k]], [bkd[h]])
                for b in range(NB):
                    sl = slice(b * 128, (b + 1) * 128)
                    TS("dve", ki[:, h, sl], kd[:, h, sl], s2[:, h, b:b + 1], None, ALU.mult, None, [bkd[h], bsm[h]], [bki[h]])
                yield
            if ti == 0 and heads[-1] == 3:
                DBG("qi", qi, bqi, BF16)
                DBG("qd", qd, bqd, BF16)
                DBG("kd", kd, bkd, BF16)
                DBG("ki", ki, bki, BF16)
                DBG("dec", dec[:], bsm)

        hg_bkU = [None] * NB

        def hgA_stream(ti, blocks=(0, 1, 2, 3)):
            for b in blocks:
                sl = slice(b * 128, (b + 1) * 128)
                k = b % 2
                bkS, bbkS = nextbank()
                for h in range(4):
                    MM(bkS[:, h * 128:(h + 1) * 128], kd[:, h, sl], qd[:, h, sl], True, True, [bkd[h], bqd[h]], [bbkS])
                bkT, bbkT = nextbank()
                pv = bkT[:].bitcast(BF16)
                for h in range(4):
                    TR(pv[:, h * 128:(h + 1) * 128], ki[:, h, sl], [bki[h]], [bbkT])
                yield
                for h in range(4):
                    TT("dve", sT[k][:, h, :], bkS[:, h * 128:(h + 1) * 128], mask_st[:], ALU.mult, [bbkS, bconst], [bsT[k]])
                CP("act", kitm[k][:], pv[:, 0:512].rearrange("p (h t) -> p h t", t=128), [bbkT], [bkitm[k]])
                bkU, bbkU = nextbank(reserve=True)
                for h in range(4):
                    MM(bkU[:, h * 128:(h + 1) * 128], kitm[k][:, h, :], vtm[:, b, h * 128:(h + 1) * 128], True, True, [bkitm[k], bvtm[b]], [bbkU])
                hg_bkU[b] = (bkU, bbkU)
                yield

        def hgB_stream(ti):
            for b in range(NB):
                sl = slice(b * 128, (b + 1) * 128)
                k = b % 2
                bkU, bbkU = hg_bkU[b]
                bkO, bbkO = nextbank()
                for h in range(4):
                    MM(bkO[:, h * 128:(h + 1) * 128], vtm[:, b, h * 128:(h + 1) * 128], sT[k][:, h, :], True, False, [bvtm[b], bsT[k]], [bbkO])
                    MM(bkO[:, h * 128:(h + 1) * 128], Sb[:, h, :], qi[:, h, sl], False, True, [bSb[h], bqi[h]], [bbkO])
                for h in range(4):
                    STT(S32[:, h, :], S32[:, h, :], dec[:, h, b:b + 1], bkU[:, h * 128:(h + 1) * 128], ALU.mult, ALU.add,
                        [bS32[h], bsm[h], bbkU], [bS32[h]])
                release_bank(bkU)
                CP("dve", Sb[:], S32[:], bS32, bSb)
                A(sqo[k][:], bkO[:], AF.Square, [bbkO], [bsqo[k]])
                bkN, bbkN = nextbank()
                MM(bkN[:], ones_bf[:], sqo[k][:], True, True, [bconst, bsqo[k]], [bbkN])
                yield
                A(lnt[k][:], bkN[:], AF.Ln, [bbkN, bprm], [blnt[k]], scale=1.0 / 128, bias=epsc[:, 0:1])
                A(lnt[k][:], lnt[k][:], AF.Exp, [blnt[k]], [blnt[k]], scale=-0.5)
                STT(t1[k][:], bkO[:], og[:, 0:1], lnt[k][:], ALU.mult, ALU.mult, [bbkO, blnt[k], bprm], [bt1[k]])
                TT("pool", AT[:, :, sl], t1[k][:].rearrange("p (h t) -> p h t", t=128), gs[:, :, sl], ALU.mult, [bt1[k]] + bgs, [bAT[b]])
                yield
            if ti == 0:
                DBG("AT", AT[:], bAT, BF16)

        def swa_block_stream(ti):
            for b in range(NB):
                sl = slice(b * 128, (b + 1) * 128)
                gb = ti * NB + b
                kbs = (1,) if gb == 0 else (0, 1)
                for kv in range(2):
                    for kb in kbs:
                        bkC, bbkC = nextbank()
                        MM(bkC[:].rearrange("p (g t) -> p g t", t=128), knz[kv][:, b + kb, :], qn[:, :, sl], True, True, [bkn[b + kb]] + bqn, [bbkC])
                        e = kb % 2
                        TT("dve", Ef[e][:], bkC[:], dm[:, kv, kb, :], ALU.add, [bbkC, bconst], [bEf[e]])
                        A(Ep[kv * 2 + kb][:], Ef[e][:], AF.Exp, [bEf[e], bprm], [bEp[kv * 2 + kb]], bias=negshift[:, 0:1])
                    yield
                bkP, bbkP = nextbank()
                bkD, bbkD = nextbank()
                pairs = [(kv, kb) for kv in range(2) for kb in kbs]
                for i_, (kv, kb) in enumerate(pairs):
                    MM(bkP[:], vtz[kv][:, b + kb, :], Ep[kv * 2 + kb][:], i_ == 0, i_ == len(pairs) - 1, [bvt[b + kb], bEp[kv * 2 + kb]], [bbkP])
                for i_, (kv, kb) in enumerate(pairs):
                    MM(bkD[:], onesz[:, kv, :], Ep[kv * 2 + kb][:], i_ == 0, i_ == len(pairs) - 1, [bconst, bEp[kv * 2 + kb]], [bbkD])
                yield
                for g in range(4):
                    gsl = slice(g * 128, (g + 1) * 128)
                    A(rden[:, gsl], bkD[:, gsl], AF.Ln, [bbkD, bprm], [brden], bias=sinkexp[:, g:g + 1])
                A(rden[:], rden[:], AF.Exp, [brden], [brden], scale=-1.0)
                TT("dve", BT[:, :, sl], bkP[:].rearrange("p (g t) -> p g t", t=128), rden[:].rearrange("p (g t) -> p g t", t=128),
                   ALU.mult, [bbkP, brden], [bBT[b]])
                yield
            if ti == 0:
                DBG("BT", BT[:], bBT, BF16)

        def gates_stream(ti):
            for k4 in range(4):
                Wg, bWg = piece(ti, P_GAB[k4])
                for cc in range(2):
                    dc = 2 * k4 + cc
                    bkGa, bbkGa = fm_group(Wg, bWg, (2 * cc) * 128)
                    CP("act", mixT[:, dc, :], bkGa[:], [bbkGa], [bmix[dc]])
                    yield
                    bkGb, bbkGb = fm_group(Wg, bWg, (2 * cc + 1) * 128)
                    CP("dve", gbuf[:, dc, :], bkGb[:], [bbkGb], [bgb[dc]])
                    yield

        def branches(ti):
            for hf in range(2):
                Wab, bWab = piece(ti, P_WAB[hf])
                for c4 in range(4):
                    dc = hf * 4 + c4
                    k = dc % 2
                    A(tht[0][:], mixT[:, dc, :], AF.Tanh, [bmix[dc]], [btht[0]], scale=0.5)
                    A(tht[1][:], gbuf[:, dc, :], AF.Tanh, [bgb[dc]], [btht[1]], scale=0.5)
                    bkA, bbkA = nextbank()
                    for h in range(4):
                        MM(bkA[:], Wab[:, h * 512 + c4 * 128: h * 512 + (c4 + 1) * 128], AT[:, h, :], h == 0, h == 3, [bWab] + bAT, [bbkA])
                    bkB, bbkB = nextbank()
                    for g in range(4):
                        MM(bkB[:], Wab[:, (4 + g) * 512 + c4 * 128: (4 + g) * 512 + (c4 + 1) * 128], BT[:, g, :], g == 0, g == 3, [bWab] + bBT, [bbkB])
                    STT(m1[k][:], tht[0][:], 1.0, bkA[:], ALU.add, ALU.mult, [btht[0], bbkA], [bm1[k]])
                    STT(m2[k][:], tht[1][:], 1.0, bkB[:], ALU.add, ALU.mult, [btht[1], bbkB], [bm2[k]])
                    TT("pool", mixT[:, dc, :], m1[k][:], m2[k][:], ALU.add, [bm1[k], bm2[k]], [bmix[dc]])
            if ti == 0:
                DBG("mixT", mixT[:], bmix, BF16)

        def wout_norm2(ti):
            xb = ti % 2
            W0, bW0 = piece(ti, P_WO0)
            W1, bW1 = piece(ti, P_WO0 + 1, hold=P_WO0)
            for b in range(NB):
                for hf, (W, bW) in enumerate(((W0, bW0), (W1, bW1))):
                    bk, bbk = nextbank()
                    for dc in range(8):
                        MM(bk[:], mixT[:, dc, b * 128:(b + 1) * 128], W[:, dc * 512:(dc + 1) * 512], dc == 0, dc == 7, [bW, bmix[dc]], [bbk])
                    xsl = X[xb][:, b, hf * 512:(hf + 1) * 512]
                    STT(xsl, bk[:], 0.5, xsl, ALU.mult, ALU.add, [bbk, bX[xb][b]], [bX[xb][b]])
                norm_stats(xb, b)
                if b >= 1:
                    norm_transpose(xb, b - 1)
            norm_transpose(xb, NB - 1)
            if debug:
                for b in range(NB):
                    r0 = ti * T + b * 128
                    DMA(dbg["x1"][r0:r0 + 128, :], X[xb][:, b, :], [bX[xb][b]], [bdbg])

        def ffn_up_chunk(ti, W, bW, c, cc):
            k = c % 2
            bkG, bbkG = fm_group(W, bW, (2 * cc) * 128)
            bkV, bbkV = fm_group(W, bW, (2 * cc + 1) * 128)
            CP("act", G[k][:, 2:T + 2], bkG[:], [bbkG], [bG[k]])
            CP("pool", G[k][:, 0:2], halo[:, c, :], [bhalo[c]], [bG[k]])
            TS("dve", c1[k], G[k][:, 0:T], cw[:, 0, c:c + 1], cb[:, c:c + 1], ALU.mult, ALU.add, [bG[k], bconst], [bc1[k]])
            STT(c2[k], G[k][:, 1:T + 1], cw[:, 1, c:c + 1], c1[k], ALU.mult, ALU.add, [bG[k], bc1[k], bconst], [bc2[k]])
            STT(c1[k], bkG[:], cw[:, 2, c:c + 1], c2[k], ALU.mult, ALU.add, [bbkG, bc2[k], bconst], [bc1[k]])
            CP("pool", halo[:, c, :], G[k][:, T:T + 2], [bG[k]], [bhalo[c]])
            A(c2[k], c1[k], AF.Gelu, [bc1[k]], [bc2[k]])
            TT("dve", act[:, c, :], c2[k], bkV[:], ALU.mult, [bc2[k], bbkV], [bact[c]])

        def ffn_up(ti, pieces):
            for p in pieces:
                W, bW = piece(ti, P_UP0 + p)
                for cc in range(2):
                    ffn_up_chunk(ti, W, bW, 2 * p + cc, cc)

        def ffn_down_stream(ti, hold_first=None):
            xb = ti % 2
            for hf in range(2):
                bks = [nextbank(reserve=True) for _ in range(NB)]
                for kgp in range(3):
                    W, bW = piece(ti, P_DN0 + hf * 3 + kgp, hold=hold_first if (hf == 0 and kgp == 0) else None)
                    nk = 8 if kgp < 2 else NFC - 16
                    for b in range(NB):
                        for kc in range(nk):
                            c = kgp * 8 + kc
                            MM(bks[b][0][:], act[:, c, b * 128:(b + 1) * 128], W[:, kc * 512:(kc + 1) * 512], c == 0, c == NFC - 1, [bW, bact[c]], [bks[b][1]])
                        if not (hf == 1 and kgp == 2):
                            yield
                for b in range(NB):
                    xsl = X[xb][:, b, hf * 512:(hf + 1) * 512]
                    TT("dve", xsl, bks[b][0][:], xsl, ALU.add, [bks[b][1], bX[xb][b]], [bX[xb][b]])
                    release_bank(bks[b][0])
            for b in range(NB):
                r0 = ti * T + b * 128
                DMA(y_d[r0:r0 + 128, :], X[xb][:, b, :], [bX[xb][b]], [by_blk[xb][b]], sem_buf=bX[xb][b])

        load_x(0)
        run(norm1_stream(0))
        for ti in range(ntile):
            if ti + 1 < ntile:
                load_x(ti + 1)
            if ti > 0:
                for kv in range(2):
                    CP("pool", knz[kv][:, 0, :], knz[kv][:, NB, :], [bkn[NB]], [bkn[0]])
                    CP("pool", vtz[kv][:, 0, :], vtz[kv][:, NB, :], [bvt[NB]], [bvt[0]])
            phase_tanh(ti)
            merge(proj2_stream(ti), algebra_stream(ti, (0, 2)), algebra_stream(ti, (1, 3)))
            merge(hgA_stream(ti), hgB_stream(ti), swa_block_stream(ti), gates_stream(ti), delay=[0, 2, 0, 0])
            branches(ti)
            wout_norm2(ti)
            ffn_up(ti, range(10))
            W10, bW10 = piece(ti, P_UP0 + 10)
            gd = ffn_down_stream(ti, hold_first=P_UP0 + 10)
            for _ in range(NB):
                next(gd)
            ffn_up_chunk(ti, W10, bW10, 20, 0)
            ffn_up_chunk(ti, W10, bW10, 21, 1)
            if ti + 1 < ntile:
                merge(gd, norm1_stream(ti + 1), ratio=[3, 1])
            else:
                run(gd)
        fin = mk("fin")
        P.op("sp", lambda h: h.nop(), [b_ for l_ in by_blk for b_ in l_] + [bdbg], [fin])
        P.finalize()

        esems = {e: es.enter_context(nc.semaphore("sem_" + e)) for e in ENGS}
        dsems = [es.enter_context(nc.semaphore(f"dsem{i}")) for i in range(len(P.dma_sem_counts))]
        block = es.enter_context(nc.Block())

        @block.sync
        def _(h):
            P.emit("sp", h, esems, dsems)

        @block.scalar
        def _(h):
            P.emit("act", h, esems, dsems)

        @block.vector
        def _(h):
            P.emit("dve", h, esems, dsems)

        @block.gpsimd
        def _(h):
            P.emit("pool", h, esems, dsems)

        @block.tensor
        def _(h):
            P.emit("pe", h, esems, dsems)
    build_program.dbg_list = dbg_list
    build_program.stats = dict(sbuf_bytes=sb_total[0], n_ops={e: len(P.ops[e]) for e in ENGS}, n_dsems=len(P.dma_sem_counts))
    return nc


_CONST_KEYS = ("ident", "mask_st", "ones_bf", "blk64", "onesz", "scanmask", "dm")


def _in_maps(inputs, seq, n_cores):
    consts = _host_consts()
    x = np.asarray(inputs["x"], dtype=np.float32)
    shared = {}
    for k in ("norm1_g", "w_in", "hgrn_out_g", "q_norm_g", "k_norm_g", "attn_sinks", "w_branch_a", "w_branch_b",
              "w_out", "norm2_g", "w_up", "conv_w", "conv_b", "w_down"):
        a = np.asarray(inputs[k], dtype=np.float32)
        shared[k] = np.ascontiguousarray(a[0])
    shared["hgrn_lb_logits"] = np.ascontiguousarray(np.asarray(inputs["hgrn_lb_logits"], dtype=np.float32))
    for k in _CONST_KEYS:
        shared["c_" + k] = consts[k]
    maps = []
    for c in range(n_cores):
        m = dict(shared)
        m["x"] = np.ascontiguousarray(x[c, :seq])
        maps.append(m)
    return maps


def kernel(**inputs):
    nc = build_program(SEQ)
    maps = _in_maps(inputs, SEQ, N_CORES)
    res = run_bass_kernel_spmd(nc, maps, core_ids=list(range(N_CORES)))
    out = np.stack([np.asarray(r["y"], dtype=np.float32) for r in res.results], axis=0)
    return out
```
